# Optimizing a Trainium2 kernel written in Bass

```python
import jax, jax.numpy as jnp
from jax import lax
import numpy as np

D_MODEL = 4096
BATCH = 4
SEQ = 4096
DEPTH = 1

MIX_WIDTH = D_MODEL
POOL_WIDTH = MIX_WIDTH // 2
CONV_WIDTH = MIX_WIDTH - POOL_WIDTH
POOL_WINDOWS = (2, 4, 8, 16)
N_POOL_GROUPS = len(POOL_WINDOWS)
POOL_GROUP_DIM = POOL_WIDTH // N_POOL_GROUPS
CONV_HEAD_DIM = 128
CONV_HEADS = CONV_WIDTH // CONV_HEAD_DIM
CONV_WIDTH_K = 3
IN_PROJ_WIDTH = POOL_WIDTH + 3 * CONV_WIDTH
D_FF = 4 * D_MODEL
N_MOD = 6
EPS = 1e-6

kernel_name = "hybrid_pool_shortconv_adaln_block"


def rmsnorm(x, g):
    xf = x.astype(jnp.float32)
    xn = xf * lax.rsqrt(jnp.mean(xf * xf, axis=-1, keepdims=True) + EPS)
    return xn.astype(x.dtype) * g


def group_rmsnorm(x, g, n_groups):
    b, s, w = x.shape
    xg = x.reshape(b, s, n_groups, w // n_groups).astype(jnp.float32)
    xn = xg * lax.rsqrt(jnp.mean(xg * xg, axis=-1, keepdims=True) + EPS)
    return xn.reshape(b, s, w).astype(x.dtype) * g


def modulate(h, shift, scale):
    return h * (1 + scale[:, None, :]) + shift[:, None, :]


def multiscale_pool(v):
    b, s, _ = v.shape
    vg = v.reshape(b, s, N_POOL_GROUPS, POOL_GROUP_DIM)
    cs = jnp.cumsum(vg.astype(jnp.float32), axis=1)
    cs = jnp.pad(cs, ((0, 0), (1, 0), (0, 0), (0, 0)))
    half = jnp.array(POOL_WINDOWS, dtype=jnp.int32) // 2
    t = jnp.arange(s, dtype=jnp.int32)[:, None]
    lo = jnp.clip(t - half[None, :], 0, s)
    hi = jnp.clip(t + half[None, :], 0, s)
    gidx = jnp.arange(N_POOL_GROUPS, dtype=jnp.int32)[None, :]
    win_sum = cs[:, hi, gidx, :] - cs[:, lo, gidx, :]
    count = (hi - lo).astype(jnp.float32)[None, :, :, None]
    out = win_sum / count - vg.astype(jnp.float32)
    return out.astype(v.dtype)


def depthwise_conv3_centred(u, w, bias):
    up = jnp.pad(u, ((0, 0), (1, 1), (0, 0)))
    return w[0] * up[:, :-2] + w[1] * up[:, 1:-1] + w[2] * up[:, 2:] + bias


def setup_inputs(seed: int = 0) -> dict:
    key = jax.random.key(seed)
    ks = jax.random.split(key, 20)
    f32 = jnp.float32
    L, D = DEPTH, D_MODEL

    def nrm(k, shape, fan_in):
        return jax.random.normal(k, shape, f32) * (fan_in ** -0.5)

    def gain(k, shape):
        return 1.0 + 0.1 * jax.random.normal(k, shape, f32)

    return {
        "x": jax.random.normal(ks[0], (BATCH, SEQ, D), f32),
        "c": jax.random.normal(ks[1], (BATCH, D), f32),
        "w_ada": nrm(ks[2], (L, D, N_MOD * D), D) * 0.5,
        "b_ada": 0.02 * jax.random.normal(ks[3], (L, N_MOD * D), f32),
        "norm1_g": gain(ks[4], (L, D)),
        "w_in": nrm(ks[5], (L, D, IN_PROJ_WIDTH), D),
        "pool_mix_w": nrm(ks[6], (L, N_POOL_GROUPS, POOL_GROUP_DIM, POOL_GROUP_DIM), POOL_GROUP_DIM),
        "pool_scale": gain(ks[7], (L, POOL_WIDTH)),
        "conv_w": nrm(ks[8], (L, CONV_WIDTH_K, CONV_WIDTH), CONV_WIDTH_K),
        "conv_b": 0.02 * jax.random.normal(ks[9], (L, CONV_WIDTH), f32),
        "gnorm_pool_g": gain(ks[10], (L, POOL_WIDTH)),
        "gnorm_conv_g": gain(ks[11], (L, CONV_WIDTH)),
        "w_out": nrm(ks[12], (L, MIX_WIDTH, D), MIX_WIDTH),
        "norm2_g": gain(ks[13], (L, D)),
        "w_mlp_in": nrm(ks[14], (L, D, D_FF), D),
        "w_mlp_out": nrm(ks[15], (L, D_FF, D), D_FF),
        "final_g": gain(ks[16], (D,)),
    }


def reference(x, c, w_ada, b_ada, norm1_g, w_in, pool_mix_w, pool_scale, conv_w, conv_b,
              gnorm_pool_g, gnorm_conv_g, w_out, norm2_g, w_mlp_in, w_mlp_out, final_g):
    c_act = jax.nn.silu(c)
    for l in range(DEPTH):
        mod = c_act @ w_ada[l] + b_ada[l]
        shift1, scale1, gate1, shift2, scale2, gate2 = jnp.split(mod, N_MOD, axis=-1)

        h = modulate(rmsnorm(x, norm1_g[l]), shift1, scale1)
        proj = jnp.einsum("bsd,de->bse", h, w_in[l])
        v_pool = proj[..., :POOL_WIDTH]
        b_gate, c_gate, u = jnp.split(proj[..., POOL_WIDTH:], 3, axis=-1)

        pooled = multiscale_pool(v_pool)
        a_out = jnp.einsum("bsgd,gde->bsge", pooled, pool_mix_w[l])
        a_out = a_out.reshape(x.shape[0], x.shape[1], POOL_WIDTH) * pool_scale[l]

        b_out = b_gate * depthwise_conv3_centred(c_gate * u, conv_w[l], conv_b[l])

        mixed = jnp.concatenate(
            [group_rmsnorm(a_out, gnorm_pool_g[l], N_POOL_GROUPS),
             group_rmsnorm(b_out, gnorm_conv_g[l], CONV_HEADS)], axis=-1)
        x = x + gate1[:, None, :] * jnp.einsum("bse,ed->bsd", mixed, w_out[l])

        h = modulate(rmsnorm(x, norm2_g[l]), shift2, scale2)
        hid = jnp.square(jax.nn.relu(jnp.einsum("bsd,df->bsf", h, w_mlp_in[l])))
        x = x + gate2[:, None, :] * jnp.einsum("bsf,fd->bsd", hid, w_mlp_out[l])

    return rmsnorm(x, final_g)
```

```python
import contextlib
import numpy as np
import concourse.bass as bass
import concourse.mybir as mybir
from concourse.bass_utils import run_bass_kernel_spmd

F32 = mybir.dt.float32
BF16 = mybir.dt.bfloat16
AF = mybir.ActivationFunctionType
ALU = mybir.AluOpType

D = 4096
S = 4096
NB = 4
NCORE = 8
TOK = 2048
T = 512
NT_FULL = TOK // T
KC = D // 128
DFF = 4 * D
EPS = 1e-6
WINS = (2, 4, 8, 16)
NW = 4
NPAR = 400
O_BADA, O_G1, O_G2, O_GF, O_PSC, O_GPOOL, O_CW, O_CB, O_GCONV = 0, 192, 224, 256, 288, 304, 320, 368, 384


class Region:
    __slots__ = ("w", "r")

    def __init__(self):
        self.w = None
        self.r = {}


class Buf:
    __slots__ = ("regs", "excl")

    def __init__(self, regs, excl=False):
        self.regs = regs if isinstance(regs, (list, tuple)) else [regs]
        self.excl = excl


class Prog:
    ENGS = ("pe", "act", "dve", "pool", "sp")

    def __init__(self, nc, dry=False):
        self.nc = nc
        self.dry = dry
        self.lists = {e: [] for e in self.ENGS}
        self.cnt = {e: 0 for e in self.ENGS}
        self.known = {e: {} for e in self.ENGS}
        self.clock = {}
        self.dma_cnt = {}

    def _deps(self, eng, reads, writes):
        needs = {}

        def need(kv):
            if kv is not None and needs.get(kv[0], 0) < kv[1]:
                needs[kv[0]] = kv[1]
        for b in reads:
            for rg in b.regs:
                need(rg.w)
        for b in writes:
            for rg in b.regs:
                need(rg.w)
                for kv in rg.r.items():
                    need(kv)
        kn = self.known[eng]
        waits = []
        for k, v in needs.items():
            if k == eng and eng in ("pe", "sp"):
                continue
            if kn.get(k, 0) >= v:
                continue
            waits.append((k, v))
        for k, v in waits:
            snap = self.clock.get((k, v))
            if snap:
                for kk, vv in snap.items():
                    if kn.get(kk, 0) < vv:
                        kn[kk] = vv
            if kn.get(k, 0) < v:
                kn[k] = v
        return waits

    def op(self, eng, fn, reads=(), writes=(), inc=True):
        if self.dry:
            return
        if any(b.excl for b in reads):
            writes = list(writes) + [b for b in reads if b.excl]
            reads = [b for b in reads if not b.excl]
        waits = self._deps(eng, reads, writes)
        self.lists[eng].append((waits, fn, inc))
        if inc:
            self.cnt[eng] += 1
            pos = self.cnt[eng]
            self.clock[(eng, pos)] = dict(self.known[eng])
        else:
            pos = self.cnt[eng] + 1
        for b in reads:
            for rg in b.regs:
                if rg.r.get(eng, 0) < pos:
                    rg.r[eng] = pos
        for b in writes:
            for rg in b.regs:
                rg.w = (eng, pos)
                rg.r = {}

    def dma(self, q, slot, out_ap, in_ap, reads=(), writes=()):
        if self.dry:
            return None
        waits = self._deps(q, reads, writes)
        n = self.dma_cnt.get(slot, 0) + 1
        self.dma_cnt[slot] = n
        key, val = ("dma", slot), 16 * n
        self.lists[q].append((waits, (out_ap, in_ap, slot), "dma"))
        self.clock[(key, val)] = dict(self.known[q])
        for b in reads:
            for rg in b.regs:
                if rg.r.get(key, 0) < val:
                    rg.r[key] = val
        for b in writes:
            for rg in b.regs:
                rg.w = (key, val)
                rg.r = {}
        return key, val

    def wait_all(self, eng, kvs):
        self.lists[eng].append((list(kvs), None, "nop"))

    def emit(self):
        nc = self.nc
        with contextlib.ExitStack() as st:
            sems = {}
            for e in self.ENGS:
                sems[e] = st.enter_context(nc.semaphore("s_" + e))
            for s in sorted(self.dma_cnt.keys()):
                sems[("dma", s)] = st.enter_context(nc.semaphore("d_" + s))
            block = st.enter_context(nc.Block())

            def run(name, eng):
                own = sems[name]
                for waits, fn, kind in self.lists[name]:
                    for k, v in waits:
                        eng.wait_ge(sems[k], v)
                    if kind == "dma":
                        eng.dma_start(out=fn[0], in_=fn[1]).then_inc(sems[("dma", fn[2])], 16)
                    elif kind == "nop":
                        pass
                    else:
                        ins = fn(eng)
                        if kind:
                            ins.then_inc(own, 1)

            @block.tensor
            def _(e):
                run("pe", e)

            @block.scalar
            def _(e):
                run("act", e)

            @block.vector
            def _(e):
                run("dve", e)

            @block.gpsimd
            def _(e):
                run("pool", e)

            @block.sync
            def _(e):
                run("sp", e)


def build_nc(NT):
    nc = bass.Bass("TRN2", target_bir_lowering=False)

    def din(name, shape):
        return nc.dram_tensor(name, shape, F32, kind="ExternalInput").ap()
    xeT = din("xeT", [D, TOK + 16])
    ctd = din("ct", [128, KC])
    pard = din("par", [128, NPAR])
    tabd = din("tab", [128, NT_FULL * 80])
    identd = din("ident", [128, 128])
    w_ada = din("w_ada", [D, 6 * D])
    w_in = din("w_in", [D, 2 * D])
    pmw = din("pmw", [2048, 512])
    w_out = din("w_out", [D, D])
    w1 = din("w1", [D, DFF])
    w2 = din("w2", [DFF, D])
    yT = nc.dram_tensor("yT", [D, TOK], F32, kind="ExternalOutput").ap()
    xv = xeT.rearrange("(c p) t -> p c t", p=128)
    yv = yT.rearrange("(c p) t -> p c t", p=128)
    NWT = 64 + 4 + 32 + 256
    wsc_parts = [nc.dram_tensor(f"wsc{i}", [NWT // 2, 128, KC * 128], BF16, kind="Internal").ap() for i in range(2)]

    class _Wsc:
        def __getitem__(self, i):
            return wsc_parts[i // (NWT // 2)][i % (NWT // 2)]
    wsc = _Wsc()

    def wview(w):
        return w.rearrange("(k p) e -> p k e", p=128)
    v_ada, v_in, v_pm, v_out, v_w1, v_w2 = (wview(w) for w in (w_ada, w_in, pmw, w_out, w1, w2))

    with contextlib.ExitStack() as st:
        def sb(name, shape, dt):
            return st.enter_context(nc.sbuf_tensor("sb_" + name, shape, dt))

        xres = sb("xres", [128, KC, T], F32)
        xh = sb("xh", [128, KC, 16], F32)
        h = sb("h", [128, KC, T], BF16)
        hh = sb("hh", [128, KC, 16], BF16)
        mh = sb("mh", [128, KC, T], BF16)
        wbf = [sb(f"wbf{i}", [128, KC, 128], BF16) for i in range(NW)]
        big = sb("big", [128, 2048], F32)
        xst = [big[:, i * 1024:(i + 1) * 1024] for i in range(2)]
        vext = [sb(f"vext{i}", [128, 528], F32) for i in range(2)]
        sA = sb("sA", [128, 528], F32)
        sB = sb("sB", [128, 528], F32)
        pooled = sb("pooled", [128, 4, T], BF16)
        a_g = big[:].rearrange("p (m t) -> p m t", m=4)
        cext = sA[:, 0:514]
        cuext = sB[:, 0:514]
        acc = vext[0][:, 0:T]
        Bo = vext[1][:, 0:T]
        sqbf = [sb(f"sqbf{i}", [128, T], BF16) for i in range(2)]
        sqh = sb("sqh", [128, KC, 16], BF16)
        rstd = [sb(f"rstd{i}", [128, T], F32) for i in range(2)]
        rstdh = sb("rstdh", [128, 16], F32)
        tmp = [sb(f"tmp{i}", [128, T], F32) for i in range(2)]
        tmph = sb("tmph", [128, KC, 16], F32)
        par = sb("par", [128, NPAR], F32)
        tab = sb("tab", [128, NT_FULL * 80], F32)
        ident = sb("ident", [128, 128], F32)
        ones = sb("ones", [128, 128], BF16)
        epsc = sb("epsc", [128, 1], F32)
        ct = sb("ct", [128, KC], F32)
        sbf = sb("sbf", [128, KC], BF16)
        mod = sb("mod", [128, 192], F32)
        gm = sb("gm", [128, 64], F32)
        banks = [st.enter_context(nc.psum_tensor(f"bank{i}", [128, 512], F32)) for i in range(8)]

        def build(P, wlist):
            def B1():
                return Buf(Region())
            R_xres = [Region() for _ in range(KC)]
            R_h = [Region() for _ in range(KC)]
            R_mh = [Region() for _ in range(KC)]
            b_xres = [Buf(r) for r in R_xres]
            b_h = [Buf(r) for r in R_h]
            b_mh = [Buf(r) for r in R_mh]
            b_xres_all, b_h_all, b_mh_all = Buf(R_xres), Buf(R_h), Buf(R_mh)
            b_xh, b_hh = B1(), B1()
            b_wbf = [B1() for _ in range(NW)]
            b_xst = [B1() for _ in range(2)]
            b_vext = [B1(), B1()]
            b_sA, b_sB, b_pooled = B1(), B1(), B1()
            b_cext, b_cuext, b_acc, b_Bo = b_sA, b_sB, b_vext[0], b_vext[1]
            b_ag = [b_xst[0], b_xst[0], b_xst[1], b_xst[1]]
            b_sq = [B1(), B1()]
            b_sqh, b_rstdh, b_tmph = B1(), B1(), B1()
            b_rstd = [B1(), B1()]
            b_tmp = [B1(), B1()]
            b_par, b_tab, b_ident, b_ones, b_eps, b_ct, b_sbf, b_mod, b_gm = (B1() for _ in range(9))
            b_bank = [Buf(Region(), excl=True) for _ in range(8)]
            rot = {"m": 0, "h": 0, "s": 0, "sq": 0, "rs": 0, "tp": 0, "ev": 0, "sqe": 0}

            def nxt(kind, n):
                v = rot[kind]
                rot[kind] = (v + 1) % n
                return v

            def main_bank():
                return nxt("m", 4)

            def halo_bank():
                return 4 + nxt("h", 2)

            def stat_bank():
                return 6 + nxt("s", 2)

            wstate = {"use": 0, "iss": 0}

            b_wsc = [B1() for _ in range(NWT)]
            wi = [0]

            def issue_w(n):
                view, nsl, sc = wlist[n]
                slot = n % NW
                ep = (n // NW) // 150
                flat = wbf[slot][:, 0:nsl, :].rearrange("p k c -> p (k c)")
                wb_at = 0 if sc is None else min(sc[1] % 3, max(NT - 2, 0))
                if sc is None or sc[0] <= wb_at:
                    out = wbf[slot][:, 0:nsl, :]
                    if view.shape[2] != 128:
                        out = wbf[slot][:, 0:nsl, :].rearrange("p (k m) c -> p k (m c)", k=view.shape[1])
                    P.dma("pool", f"w{slot}e{ep}", out, view, writes=[b_wbf[slot]])
                    if sc is not None and NT > 1 and sc[0] == wb_at:
                        P.dma("sp", f"wb{slot}e{ep}", wsc[sc[1]][:, 0:nsl * 128], flat, reads=[b_wbf[slot]], writes=[b_wsc[sc[1]]])
                else:
                    P.dma("pool", f"w{slot}e{ep}", flat, wsc[sc[1]][:, 0:nsl * 128], reads=[b_wsc[sc[1]]], writes=[b_wbf[slot]])

            def next_w(view, nsl=KC, tile_it=None):
                n = wstate["use"]
                wstate["use"] += 1
                sc = None
                if tile_it is not None:
                    sc = (tile_it, wi[0])
                    wi[0] += 1
                if P.dry:
                    wlist.append((view, nsl, sc))
                    return 0
                while wstate["iss"] < min(len(wlist), n + NW):
                    issue_w(wstate["iss"])
                    wstate["iss"] += 1
                return n % NW

            def evac_eng():
                return ("act", "dve")[nxt("ev", 2)]

            def copy_op(eng, out_ap, in_ap, reads, writes):
                if eng == "act":
                    P.op("act", lambda e: e.copy(out_ap, in_ap), reads=reads, writes=writes)
                else:
                    P.op("dve", lambda e: e.tensor_copy(out_ap, in_ap), reads=reads, writes=writes)

            def rsqrt_from_bank(bk, ncol, scale, dst_ap, dst_buf):
                P.op("act", lambda e: e.activation(dst_ap, banks[bk][:, 0:ncol], AF.Sqrt, bias=epsc[:, 0:1], scale=scale),
                     reads=[b_bank[bk], b_eps], writes=[dst_buf])
                P.op("dve", lambda e: e.reciprocal(dst_ap, dst_ap), reads=[dst_buf], writes=[dst_buf])

            P.dma("sp", "par", par[:], pard, writes=[b_par])
            P.dma("sp", "ct", ct[:], ctd, writes=[b_ct])
            P.dma("sp", "tab", tab[:], tabd, writes=[b_tab])
            P.dma("sp", "ident", ident[:], identd, writes=[b_ident])
            P.op("dve", lambda e: e.memset(ones[:], 1.0), writes=[b_ones])
            P.op("dve", lambda e: e.memset(epsc[:], EPS), writes=[b_eps])
            P.op("act", lambda e: e.activation(sbf[:], ct[:], AF.Silu), reads=[b_ct], writes=[b_sbf])
            MODB = 6
            hf32 = h[:].rearrange("p k t -> p (k t)").bitcast(F32)
            mf32 = mh[:].rearrange("p k t -> p (k t)").bitcast(F32)
            stg = [hf32[:, 0:4096].rearrange("p (k c) -> p k c", k=8), hf32[:, 4096:8192].rearrange("p (k c) -> p k c", k=8),
                   mf32[:, 0:4096].rearrange("p (k c) -> p k c", k=8), mf32[:, 4096:8192].rearrange("p (k c) -> p k c", k=8)]
            b_stg = [Buf(R_h[0:16]), Buf(R_h[16:32]), Buf(R_mh[0:16]), Buf(R_mh[16:32])]
            cast_rot = ("act", "dve", "pool", "dve", "act", "pool", "dve")
            for cb in range(48):
                for kq in range(4):
                    n = cb * 4 + kq
                    sl = n % 4
                    P.dma("sp", f"ada{sl}", stg[sl], v_ada[:, kq * 8:(kq + 1) * 8, cb * 512:(cb + 1) * 512], writes=[b_stg[sl]])
                    wv = wbf[sl][:].rearrange("p (k m) c -> p k (m c)", k=8)
                    ce = cast_rot[n % len(cast_rot)]
                    if ce == "act":
                        P.op("act", lambda e, sl=sl, wv=wv: e.copy(wv, stg[sl]), reads=[b_stg[sl]], writes=[b_wbf[sl]])
                    else:
                        P.op(ce, lambda e, sl=sl, wv=wv: e.tensor_copy(wv, stg[sl]), reads=[b_stg[sl]], writes=[b_wbf[sl]])
                    for m in range(4):
                        for k in range(8):
                            col = cb * 4 + m
                            kk = kq * 8 + k
                            P.op("pe", lambda e, wv=wv, m=m, k=k, col=col, kk=kk, n=n, kq=kq: e.matmul(
                                banks[MODB][:, col:col + 1], wv[:, k, m * 128:(m + 1) * 128], sbf[:, kk:kk + 1],
                                start=(n == 0 and m == 0 and k == 0), stop=(kq == 3 and k == 7), skip_group_check=True),
                                 reads=[b_wbf[sl], b_sbf], writes=[b_bank[MODB]], inc=(m == 3 and k == 7))
            P.op("dve", lambda e: e.tensor_tensor(mod[:], banks[MODB][:, 0:192], par[:, O_BADA:O_BADA + 192], ALU.add),
                 reads=[b_bank[MODB], b_par], writes=[b_mod])
            P.op("dve", lambda e: e.scalar_tensor_tensor(gm[:, 0:32], mod[:, 32:64], 1.0, par[:, O_G1:O_G1 + 32], ALU.add, ALU.mult),
                 reads=[b_mod, b_par], writes=[b_gm])
            P.op("dve", lambda e: e.scalar_tensor_tensor(gm[:, 32:64], mod[:, 128:160], 1.0, par[:, O_G2:O_G2 + 32], ALU.add, ALU.mult),
                 reads=[b_mod, b_par, b_gm], writes=[b_gm])

            def norm_stats(src_bufs, halo):
                sbk = stat_bank()
                for c in range(KC):
                    q = nxt("sq", 2)
                    P.op("act", lambda e, c=c, q=q: e.activation(sqbf[q][:], xres[:, c, :], AF.Square),
                         reads=[src_bufs[c]], writes=[b_sq[q]])
                    P.op("pe", lambda e, c=c, q=q: e.matmul(banks[sbk][:], ones[:], sqbf[q][:], start=(c == 0), stop=(c == KC - 1)),
                         reads=[b_sq[q], b_ones], writes=[b_bank[sbk]], inc=True)
                hbk = None
                if halo:
                    hbk = halo_bank()
                    P.op("act", lambda e: e.activation(sqh[:], xh[:], AF.Square), reads=[b_xh], writes=[b_sqh])
                    for c in range(KC):
                        P.op("pe", lambda e, c=c: e.matmul(banks[hbk][:, 0:16], ones[:], sqh[:, c, :], start=(c == 0), stop=(c == KC - 1)),
                             reads=[b_sqh, b_ones], writes=[b_bank[hbk]], inc=(c == KC - 1))
                return sbk, hbk

            def stat_chunk(sbk, c, first, last):
                q = nxt("sq", 2)
                if nxt("sqe", 2) == 0:
                    P.op("act", lambda e, c=c, q=q: e.activation(sqbf[q][:], xres[:, c, :], AF.Square),
                         reads=[b_xres[c]], writes=[b_sq[q]])
                else:
                    P.op("dve", lambda e, c=c, q=q: e.tensor_tensor(sqbf[q][:], xres[:, c, :], xres[:, c, :], ALU.mult),
                         reads=[b_xres[c]], writes=[b_sq[q]])
                P.op("pe", lambda e, q=q: e.matmul(banks[sbk][:], ones[:], sqbf[q][:], start=first, stop=last),
                     reads=[b_sq[q], b_ones], writes=[b_bank[sbk]], inc=True)

            out_dmas = []
            def tile_body(it):
                s0 = it * T
                wi[0] = 0
                o_hm = it * 16
                o_corr = NT_FULL * 16 + it * 64

                for g8 in range(4):
                    P.dma("sp", f"xl{g8}", xres[:, 8 * g8:8 * (g8 + 1), :], xv[:, 8 * g8:8 * (g8 + 1), 8 + s0:8 + s0 + T],
                          writes=[Buf(R_xres[8 * g8:8 * (g8 + 1)])])
                P.dma("sp", "xlh", xh[:, :, 0:8], xv[:, :, s0:s0 + 8], writes=[b_xh])
                P.dma("sp", "xlh", xh[:, :, 8:16], xv[:, :, 8 + s0 + T:16 + s0 + T], writes=[b_xh])

                sbk = stat_bank()
                for c in range(KC):
                    stat_chunk(sbk, c, first=(c == 0), last=(c == KC - 1))
                hbk = halo_bank()
                P.op("act", lambda e: e.activation(sqh[:], xh[:], AF.Square), reads=[b_xh], writes=[b_sqh])
                for c in range(KC):
                    P.op("pe", lambda e, c=c: e.matmul(banks[hbk][:, 0:16], ones[:], sqh[:, c, :], start=(c == 0), stop=(c == KC - 1)),
                         reads=[b_sqh, b_ones], writes=[b_bank[hbk]], inc=(c == KC - 1))
                r1 = nxt("rs", 2)
                rsqrt_from_bank(sbk, T, 1.0 / D, rstd[r1][:], b_rstd[r1])
                rsqrt_from_bank(hbk, 16, 1.0 / D, rstdh[:], b_rstdh)
                for c in range(KC):
                    q = nxt("tp", 2)
                    P.op("dve", lambda e, c=c, q=q: e.tensor_tensor(tmp[q][:], xres[:, c, :], rstd[r1][:], ALU.mult),
                         reads=[b_xres[c], b_rstd[r1]], writes=[b_tmp[q]])
                    P.op("act", lambda e, c=c, q=q: e.activation(h[:, c, :], tmp[q][:], AF.Identity, bias=mod[:, c:c + 1], scale=gm[:, c:c + 1]),
                         reads=[b_tmp[q], b_mod, b_gm], writes=[b_h[c]])
                P.op("dve", lambda e: e.tensor_tensor(tmph[:], xh[:], rstdh[:].unsqueeze(1).to_broadcast([128, KC, 16]), ALU.mult),
                     reads=[b_xh, b_rstdh], writes=[b_tmph])
                P.op("dve", lambda e: e.tensor_tensor(tmph[:], tmph[:], gm[:, 0:32].unsqueeze(2).to_broadcast([128, KC, 16]), ALU.mult),
                     reads=[b_tmph, b_gm], writes=[b_tmph])
                P.op("dve", lambda e: e.tensor_tensor(tmph[:], tmph[:], mod[:, 0:32].unsqueeze(2).to_broadcast([128, KC, 16]), ALU.add),
                     reads=[b_tmph, b_mod], writes=[b_tmph])
                P.op("dve", lambda e: e.tensor_tensor(hh[:], tmph[:], tab[:, o_hm:o_hm + 16].unsqueeze(1).to_broadcast([128, KC, 16]), ALU.mult),
                     reads=[b_tmph, b_tab], writes=[b_hh])

                def inproj_group(colblk, with_halo):
                    ws = next_w(v_in[:, :, colblk * 128:(colblk + 1) * 128], KC, it)
                    bk = main_bank()
                    hb_ = halo_bank() if with_halo else None
                    for k in range(KC):
                        last = (k == KC - 1)
                        P.op("pe", lambda e, k=k, ws=ws, bk=bk: e.matmul(banks[bk][:], wbf[ws][:, k, :], h[:, k, :], start=(k == 0), stop=(k == KC - 1)),
                             reads=[b_wbf[ws], b_h[k]], writes=[b_bank[bk]], inc=(last and not with_halo))
                        if with_halo:
                            P.op("pe", lambda e, k=k, ws=ws, hb_=hb_: e.matmul(banks[hb_][:, 0:16], wbf[ws][:, k, :], hh[:, k, :], start=(k == 0), stop=(k == KC - 1)),
                                 reads=[b_wbf[ws], b_hh], writes=[b_bank[hb_]], inc=last)
                    return bk, hb_

                def pool_post(g):
                    ws = next_w(v_pm[:, 4 * g:4 * g + 4, 0:512], 16, it)
                    sbk2 = stat_bank()
                    for m2 in range(4):
                        bk = main_bank()
                        for k in range(4):
                            P.op("pe", lambda e, k=k, m2=m2, ws=ws, bk=bk: e.matmul(banks[bk][:], wbf[ws][:, k * 4 + m2, :], pooled[:, k, :], start=(k == 0), stop=(k == 3)),
                                 reads=[b_wbf[ws], b_pooled], writes=[b_bank[bk]], inc=(k == 3))
                        cc = 4 * g + m2
                        P.op("act", lambda e, m2=m2, bk=bk, cc=cc: e.activation(a_g[:, m2, :], banks[bk][:], AF.Identity, scale=par[:, O_PSC + cc:O_PSC + cc + 1]),
                             reads=[b_bank[bk], b_par], writes=[b_ag[m2]])
                        q = nxt("sq", 2)
                        P.op("act", lambda e, m2=m2, q=q: e.activation(sqbf[q][:], a_g[:, m2, :], AF.Square), reads=[b_ag[m2]], writes=[b_sq[q]])
                        P.op("pe", lambda e, m2=m2, q=q: e.matmul(banks[sbk2][:], ones[:], sqbf[q][:], start=(m2 == 0), stop=(m2 == 3)),
                             reads=[b_sq[q], b_ones], writes=[b_bank[sbk2]], inc=True)
                    r = nxt("rs", 2)
                    rsqrt_from_bank(sbk2, T, 1.0 / 512, rstd[r][:], b_rstd[r])
                    for m2 in range(4):
                        cc = 4 * g + m2
                        P.op("dve", lambda e, m2=m2, cc=cc, r=r: e.scalar_tensor_tensor(mh[:, cc, :], a_g[:, m2, :], par[:, O_GPOOL + cc:O_GPOOL + cc + 1], rstd[r][:], ALU.mult, ALU.mult),
                             reads=[b_ag[m2], b_par, b_rstd[r]], writes=[b_mh[cc]])

                pend = []
                for g in range(4):
                    w = WINS[g]
                    half = w // 2
                    for m in range(4):
                        cc = 4 * g + m
                        bk, hb_ = inproj_group(cc, True)
                        if pend and pend[0][0] <= 0:
                            pend.pop(0)[1]()
                        pend = [(n - 1, f) for n, f in pend]
                        vi = cc % 2
                        P.op("act", lambda e, vi=vi, bk=bk: e.copy(vext[vi][:, 8:520], banks[bk][:]), reads=[b_bank[bk]], writes=[b_vext[vi]])
                        P.op("act", lambda e, vi=vi, hb_=hb_: e.copy(vext[vi][:, 0:8], banks[hb_][:, 0:8]), reads=[b_bank[hb_]], writes=[b_vext[vi]])
                        P.op("act", lambda e, vi=vi, hb_=hb_: e.copy(vext[vi][:, 520:528], banks[hb_][:, 8:16]), reads=[b_bank[hb_]], writes=[b_vext[vi]])
                        src, srcb, length = vext[vi], b_vext[vi], 528
                        dsts = [(sA, b_sA), (sB, b_sB)]
                        step, di = 1, 0
                        while step < w:
                            dst, dstb = dsts[di]
                            nl = length - step
                            P.op("dve", lambda e, src=src, dst=dst, nl=nl, step=step: e.tensor_tensor(dst[:, 0:nl], src[:, 0:nl], src[:, step:step + nl], ALU.add),
                                 reads=[srcb], writes=[dstb])
                            src, srcb, length = dst, dstb, nl
                            step *= 2
                            di ^= 1
                        lo = 8 - half
                        P.op("dve", lambda e, src=src, lo=lo, g=g: e.tensor_tensor(src[:, lo:lo + 8], src[:, lo:lo + 8], tab[:, o_corr + g * 16:o_corr + g * 16 + 8], ALU.mult),
                             reads=[srcb, b_tab], writes=[srcb])
                        P.op("dve", lambda e, src=src, lo=lo, g=g: e.tensor_tensor(src[:, lo + 504:lo + 512], src[:, lo + 504:lo + 512], tab[:, o_corr + g * 16 + 8:o_corr + g * 16 + 16], ALU.mult),
                             reads=[srcb, b_tab], writes=[srcb])
                        P.op("dve", lambda e, src=src, lo=lo, m=m, vi=vi, w=w: e.scalar_tensor_tensor(pooled[:, m, :], src[:, lo:lo + T], 1.0 / w, vext[vi][:, 8:520], ALU.mult, ALU.subtract),
                             reads=[srcb, b_vext[vi]], writes=[b_pooled])
                    pool_post(g)
                def conv_post(c):
                    sbk3 = stat_bank()
                    q = nxt("sq", 2)
                    P.op("act", lambda e, q=q: e.activation(sqbf[q][:], Bo, AF.Square), reads=[b_Bo], writes=[b_sq[q]])
                    P.op("pe", lambda e, q=q: e.matmul(banks[sbk3][:], ones[:], sqbf[q][:], start=True, stop=True),
                         reads=[b_sq[q], b_ones], writes=[b_bank[sbk3]], inc=True)
                    r = nxt("rs", 2)
                    rsqrt_from_bank(sbk3, T, 1.0 / 128, rstd[r][:], b_rstd[r])
                    P.op("dve", lambda e, c=c, r=r: e.scalar_tensor_tensor(mh[:, 16 + c, :], Bo, par[:, O_GCONV + c:O_GCONV + c + 1], rstd[r][:], ALU.mult, ALU.mult),
                         reads=[b_Bo, b_par, b_rstd[r]], writes=[b_mh[16 + c]])

                for c in range(16):
                    bkC, hbC = inproj_group(32 + c, True)
                    while pend and pend[0][0] <= 0:
                        pend.pop(0)[1]()
                    pend = [(n - 1, f) for n, f in pend]
                    P.op("act", lambda e, bkC=bkC: e.copy(cext[:, 1:513], banks[bkC][:]), reads=[b_bank[bkC]], writes=[b_cext])
                    P.op("act", lambda e, hbC=hbC: e.copy(cext[:, 0:514:513], banks[hbC][:, 7:9]), reads=[b_bank[hbC]], writes=[b_cext])
                    bkU, hbU = inproj_group(48 + c, True)
                    P.op("dve", lambda e, bkU=bkU: e.tensor_tensor(cuext[:, 1:513], banks[bkU][:], cext[:, 1:513], ALU.mult),
                         reads=[b_bank[bkU], b_cext], writes=[b_cuext])
                    P.op("dve", lambda e, hbU=hbU: e.tensor_tensor(cuext[:, 0:514:513], banks[hbU][:, 7:9], cext[:, 0:514:513], ALU.mult),
                         reads=[b_bank[hbU], b_cext], writes=[b_cuext])
                    P.op("act", lambda e, c=c: e.activation(acc, cuext[:, 0:512], AF.Identity, bias=par[:, O_CB + c:O_CB + c + 1], scale=par[:, O_CW + c:O_CW + c + 1]),
                         reads=[b_cuext, b_par], writes=[b_acc])
                    P.op("dve", lambda e, c=c: e.scalar_tensor_tensor(acc, cuext[:, 1:513], par[:, O_CW + 16 + c:O_CW + 16 + c + 1], acc, ALU.mult, ALU.add),
                         reads=[b_cuext, b_par, b_acc], writes=[b_acc])
                    P.op("dve", lambda e, c=c: e.scalar_tensor_tensor(acc, cuext[:, 2:514], par[:, O_CW + 32 + c:O_CW + 32 + c + 1], acc, ALU.mult, ALU.add),
                         reads=[b_cuext, b_par, b_acc], writes=[b_acc])
                    bkB, _ = inproj_group(16 + c, False)
                    P.op("dve", lambda e, bkB=bkB: e.tensor_tensor(Bo, banks[bkB][:], acc, ALU.mult),
                         reads=[b_bank[bkB], b_acc], writes=[b_Bo])
                    pend.append((0, lambda c=c: conv_post(c)))
                while pend:
                    pend.pop(0)[1]()

                def mblock(vw, krow0, colblk0, act, act_bufs, evac):
                    for kq in range(4):
                        ws = next_w(vw[:, krow0 + kq * 8: krow0 + (kq + 1) * 8, colblk0 * 128:(colblk0 + 4) * 128], KC, it)
                        wv = wbf[ws][:].rearrange("p (k m) c -> p k (m c)", k=8)
                        for m in range(4):
                            for k in range(8):
                                kk = kq * 8 + k
                                lastg = (kq == 3 and k == 7)
                                P.op("pe", lambda e, wv=wv, m=m, k=k, kk=kk, kq=kq: e.matmul(banks[m][:], wv[:, k, m * 128:(m + 1) * 128], act[:, kk, :],
                                                                                           start=(kq == 0 and k == 0), stop=(kq == 3 and k == 7)),
                                     reads=[b_wbf[ws], act_bufs[kk]], writes=[b_bank[m]], inc=(lastg or (m == 3 and k == 7)))
                            if kq == 3:
                                evac(m)

                for mb in range(KC // 4):
                    def ev_out(m, mb=mb):
                        j = 4 * mb + m
                        P.op("dve", lambda e, j=j, m=m: e.scalar_tensor_tensor(xres[:, j, :], banks[m][:], mod[:, 64 + j:65 + j], xres[:, j, :], ALU.mult, ALU.add),
                             reads=[b_bank[m], b_mod, b_xres[j]], writes=[b_xres[j]])
                    if mb == 0:
                        sbk_n2 = stat_bank()
                    mblock(v_out, 0, 4 * mb, mh, b_mh, ev_out)
                    if mb >= 1:
                        for m in range(4):
                            stat_chunk(sbk_n2, 4 * (mb - 1) + m, first=(mb == 1 and m == 0), last=False)
                for m in range(4):
                    stat_chunk(sbk_n2, KC - 4 + m, first=False, last=(m == 3))

                sbk = sbk_n2
                r2 = nxt("rs", 2)
                rsqrt_from_bank(sbk, T, 1.0 / D, rstd[r2][:], b_rstd[r2])
                for c in range(KC):
                    q = nxt("tp", 2)
                    P.op("dve", lambda e, c=c, q=q: e.tensor_tensor(tmp[q][:], xres[:, c, :], rstd[r2][:], ALU.mult),
                         reads=[b_xres[c], b_rstd[r2]], writes=[b_tmp[q]])
                    P.op("act", lambda e, c=c, q=q: e.activation(h[:, c, :], tmp[q][:], AF.Identity, bias=mod[:, 96 + c:97 + c], scale=gm[:, 32 + c:33 + c]),
                         reads=[b_tmp[q], b_mod, b_gm], writes=[b_h[c]])

                for hb in range(DFF // D):
                    for mb in range(KC // 4):
                        def ev_w1(m, mb=mb):
                            f = 4 * mb + m
                            q = nxt("tp", 2)
                            P.op("act", lambda e, q=q, m=m: e.activation(tmp[q][:], banks[m][:], AF.Relu), reads=[b_bank[m]], writes=[b_tmp[q]])
                            P.op("dve", lambda e, q=q, f=f: e.tensor_tensor(mh[:, f, :], tmp[q][:], tmp[q][:], ALU.mult), reads=[b_tmp[q]], writes=[b_mh[f]])
                        mblock(v_w1, 0, hb * KC + 4 * mb, h, b_h, ev_w1)
                    for mb in range(KC // 4):
                        def ev_w2(m, mb=mb):
                            j = 4 * mb + m
                            P.op("dve", lambda e, j=j, m=m: e.scalar_tensor_tensor(xres[:, j, :], banks[m][:], mod[:, 160 + j:161 + j], xres[:, j, :], ALU.mult, ALU.add),
                                 reads=[b_bank[m], b_mod, b_xres[j]], writes=[b_xres[j]])
                        lasthb = (hb == DFF // D - 1)
                        if lasthb and mb == 0:
                            sbk_fin = stat_bank()
                        mblock(v_w2, hb * KC, 4 * mb, mh, b_mh, ev_w2)
                        if lasthb and mb >= 1:
                            for m in range(4):
                                stat_chunk(sbk_fin, 4 * (mb - 1) + m, first=(mb == 1 and m == 0), last=False)
                for m in range(4):
                    stat_chunk(sbk_fin, KC - 4 + m, first=False, last=(m == 3))

                sbk = sbk_fin
                r3 = nxt("rs", 2)
                rsqrt_from_bank(sbk, T, 1.0 / D, rstd[r3][:], b_rstd[r3])
                for c in range(KC):
                    P.op("dve", lambda e, c=c: e.scalar_tensor_tensor(xres[:, c, :], xres[:, c, :], par[:, O_GF + c:O_GF + c + 1], rstd[r3][:], ALU.mult, ALU.mult),
                         reads=[b_xres[c], b_par, b_rstd[r3]], writes=[b_xres[c]])
                for g4 in range(8):
                    kv = P.dma("sp", f"yo{g4}", yv[:, 4 * g4:4 * (g4 + 1), s0:s0 + T], xres[:, 4 * g4:4 * (g4 + 1), :],
                               reads=[Buf(R_xres[4 * g4:4 * (g4 + 1)])])
                    out_dmas.append(kv)
            for it_ in range(NT):
                tile_body(it_)
            if not P.dry:
                P.wait_all("sp", [kv for kv in {k: v for k, v in out_dmas}.items()])

        wlist = []
        build(Prog(nc, dry=True), wlist)
        P = Prog(nc)
        build(P, wlist)
        P.emit()
    return nc


def _fm(v, n):
    return np.ascontiguousarray(np.asarray(v, np.float32).reshape(n, 128).T)


def _tables(half):
    t0 = half * TOK
    hm = np.zeros((NT_FULL, 16), np.float32)
    corr = np.ones((NT_FULL, 4, 16), np.float32)
    for it in range(NT_FULL):
        s0 = t0 + it * T
        for j in range(16):
            tok = s0 - 8 + j if j < 8 else s0 + T + (j - 8)
            hm[it, j] = 1.0 if 0 <= tok < S else 0.0
        for g, w in enumerate(WINS):
            hf = w // 2
            for j in range(16):
                tok = s0 + j if j < 8 else s0 + T - 16 + j
                cnt = min(tok + hf, S) - max(tok - hf, 0)
                corr[it, g, j] = float(w) / float(cnt)
    row = np.concatenate([hm.reshape(-1), corr.reshape(-1)])
    return np.ascontiguousarray(np.broadcast_to(row[None, :], (128, row.size))).astype(np.float32)


_NT = NT_FULL


def kernel(x, c, w_ada, b_ada, norm1_g, w_in, pool_mix_w, pool_scale, conv_w, conv_b,
           gnorm_pool_g, gnorm_conv_g, w_out, norm2_g, w_mlp_in, w_mlp_out, final_g):
    f32 = np.float32
    x = np.asarray(x, f32)
    c = np.asarray(c, f32)
    par = np.concatenate([
        _fm(np.asarray(b_ada)[0], 192), _fm(np.asarray(norm1_g)[0], 32), _fm(np.asarray(norm2_g)[0], 32),
        _fm(np.asarray(final_g), 32), _fm(np.asarray(pool_scale)[0], 16), _fm(np.asarray(gnorm_pool_g)[0], 16),
        _fm(np.asarray(conv_w)[0, 0], 16), _fm(np.asarray(conv_w)[0, 1], 16), _fm(np.asarray(conv_w)[0, 2], 16),
        _fm(np.asarray(conv_b)[0], 16), _fm(np.asarray(gnorm_conv_g)[0], 16)], axis=1)
    par = np.ascontiguousarray(par, f32)
    assert par.shape == (128, NPAR)
    shared = {
        "par": par,
        "ident": np.eye(128, dtype=f32),
        "w_ada": np.ascontiguousarray(np.asarray(w_ada, f32)[0]),
        "w_in": np.ascontiguousarray(np.asarray(w_in, f32)[0]),
        "pmw": np.ascontiguousarray(np.asarray(pool_mix_w, f32)[0].reshape(2048, 512)),
        "w_out": np.ascontiguousarray(np.asarray(w_out, f32)[0]),
        "w1": np.ascontiguousarray(np.asarray(w_mlp_in, f32)[0]),
        "w2": np.ascontiguousarray(np.asarray(w_mlp_out, f32)[0]),
    }
    in_maps = []
    for core in range(NCORE):
        b, half = core // 2, core % 2
        t0 = half * TOK
        xe = np.zeros((TOK + 16, D), f32)
        lo, hi = max(t0 - 8, 0), min(t0 + TOK + 8, S)
        xe[lo - (t0 - 8): hi - (t0 - 8)] = x[b, lo:hi]
        m = dict(shared)
        m["xeT"] = np.ascontiguousarray(xe.T)
        m["ct"] = _fm(c[b], KC)
        m["tab"] = _tables(half)
        in_maps.append(m)
    nc = build_nc(_NT)
    res = run_bass_kernel_spmd(nc, in_maps, core_ids=list(range(NCORE)))
    out = np.zeros((NB, S, D), f32)
    for core in range(NCORE):
        b, half = core // 2, core % 2
        out[b, half * TOK:(half + 1) * TOK] = res.results[core]["yT"].T
    return out
```

```python
import contextlib
import numpy as np
import concourse.bass as bass
import concourse.mybir as mybir
from concourse.bass_utils import run_bass_kernel_spmd

F32 = mybir.dt.float32
BF16 = mybir.dt.bfloat16
AF = mybir.ActivationFunctionType
ALU = mybir.AluOpType

D = 4096
S = 4096
NB = 4
NCORE = 8
TOK = 2048
T = 512
NT_FULL = TOK // T
KC = D // 128
DFF = 4 * D
EPS = 1e-6
WINS = (2, 4, 8, 16)
NW = 4
NPAR = 400
O_BADA, O_G1, O_G2, O_GF, O_PSC, O_GPOOL, O_CW, O_CB, O_GCONV = 0, 192, 224, 256, 288, 304, 320, 368, 384


class Region:
    __slots__ = ("w", "r")

    def __init__(self):
        self.w = None
        self.r = {}


class Buf:
    __slots__ = ("regs", "excl")

    def __init__(self, regs, excl=False):
        self.regs = regs if isinstance(regs, (list, tuple)) else [regs]
        self.excl = excl


class Prog:
    ENGS = ("pe", "act", "dve", "pool", "sp")

    def __init__(self, nc, dry=False):
        self.nc = nc
        self.dry = dry
        self.lists = {e: [] for e in self.ENGS}
        self.cnt = {e: 0 for e in self.ENGS}
        self.known = {e: {} for e in self.ENGS}
        self.clock = {}
        self.dma_cnt = {}

    def _deps(self, eng, reads, writes):
        needs = {}

        def need(kv):
            if kv is not None and needs.get(kv[0], 0) < kv[1]:
                needs[kv[0]] = kv[1]
        for b in reads:
            for rg in b.regs:
                need(rg.w)
        for b in writes:
            for rg in b.regs:
                need(rg.w)
                for kv in rg.r.items():
                    need(kv)
        kn = self.known[eng]
        waits = []
        for k, v in needs.items():
            if k == eng and eng in ("pe", "sp"):
                continue
            if kn.get(k, 0) >= v:
                continue
            waits.append((k, v))
        for k, v in waits:
            snap = self.clock.get((k, v))
            if snap:
                for kk, vv in snap.items():
                    if kn.get(kk, 0) < vv:
                        kn[kk] = vv
            if kn.get(k, 0) < v:
                kn[k] = v
        return waits

    def op(self, eng, fn, reads=(), writes=(), inc=True):
        if self.dry:
            return
        if any(b.excl for b in reads):
            writes = list(writes) + [b for b in reads if b.excl]
            reads = [b for b in reads if not b.excl]
        waits = self._deps(eng, reads, writes)
        self.lists[eng].append((waits, fn, inc))
        if inc:
            self.cnt[eng] += 1
            pos = self.cnt[eng]
            self.clock[(eng, pos)] = dict(self.known[eng])
        else:
            pos = self.cnt[eng] + 1
        for b in reads:
            for rg in b.regs:
                if rg.r.get(eng, 0) < pos:
                    rg.r[eng] = pos
        for b in writes:
            for rg in b.regs:
                rg.w = (eng, pos)
                rg.r = {}

    def dma(self, q, slot, out_ap, in_ap, reads=(), writes=()):
        if self.dry:
            return None
        waits = self._deps(q, reads, writes)
        n = self.dma_cnt.get(slot, 0) + 1
        self.dma_cnt[slot] = n
        key, val = ("dma", slot), 16 * n
        self.lists[q].append((waits, (out_ap, in_ap, slot), "dma"))
        self.clock[(key, val)] = dict(self.known[q])
        for b in reads:
            for rg in b.regs:
                if rg.r.get(key, 0) < val:
                    rg.r[key] = val
        for b in writes:
            for rg in b.regs:
                rg.w = (key, val)
                rg.r = {}
        return key, val

    def wait_all(self, eng, kvs):
        self.lists[eng].append((list(kvs), None, "nop"))

    def emit(self):
        nc = self.nc
        with contextlib.ExitStack() as st:
            sems = {}
            for e in self.ENGS:
                sems[e] = st.enter_context(nc.semaphore("s_" + e))
            for s in sorted(self.dma_cnt.keys()):
                sems[("dma", s)] = st.enter_context(nc.semaphore("d_" + s))
            block = st.enter_context(nc.Block())

            def run(name, eng):
                own = sems[name]
                for waits, fn, kind in self.lists[name]:
                    for k, v in waits:
                        eng.wait_ge(sems[k], v)
                    if kind == "dma":
                        eng.dma_start(out=fn[0], in_=fn[1]).then_inc(sems[("dma", fn[2])], 16)
                    elif kind == "nop":
                        pass
                    else:
                        ins = fn(eng)
                        if kind:
                            ins.then_inc(own, 1)

            @block.tensor
            def _(e):
                run("pe", e)

            @block.scalar
            def _(e):
                run("act", e)

            @block.vector
            def _(e):
                run("dve", e)

            @block.gpsimd
            def _(e):
                run("pool", e)

            @block.sync
            def _(e):
                run("sp", e)


def build_nc(NT):
    nc = bass.Bass("TRN2", target_bir_lowering=False)

    def din(name, shape):
        return nc.dram_tensor(name, shape, F32, kind="ExternalInput").ap()
    xeT = din("xeT", [D, TOK + 16])
    ctd = din("ct", [128, KC])
    pard = din("par", [128, NPAR])
    tabd = din("tab", [128, NT_FULL * 80])
    identd = din("ident", [128, 128])
    w_ada = din("w_ada", [D, 6 * D])
    w_in = din("w_in", [D, 2 * D])
    pmw = din("pmw", [2048, 512])
    w_out = din("w_out", [D, D])
    w1 = din("w1", [D, DFF])
    w2 = din("w2", [DFF, D])
    yT = nc.dram_tensor("yT", [D, TOK], F32, kind="ExternalOutput").ap()
    xv = xeT.rearrange("(c p) t -> p c t", p=128)
    yv = yT.rearrange("(c p) t -> p c t", p=128)
    NWT = 64 + 4 + 32 + 256
    wsc_parts = [nc.dram_tensor(f"wsc{i}", [NWT // 2, 128, KC * 128], BF16, kind="Internal").ap() for i in range(2)]

    class _Wsc:
        def __getitem__(self, i):
            return wsc_parts[i // (NWT // 2)][i % (NWT // 2)]
    wsc = _Wsc()

    def wview(w):
        return w.rearrange("(k p) e -> p k e", p=128)
    v_ada, v_in, v_pm, v_out, v_w1, v_w2 = (wview(w) for w in (w_ada, w_in, pmw, w_out, w1, w2))

    with contextlib.ExitStack() as st:
        def sb(name, shape, dt):
            return st.enter_context(nc.sbuf_tensor("sb_" + name, shape, dt))

        xres = sb("xres", [128, KC, T], F32)
        xh = sb("xh", [128, KC, 16], F32)
        h = sb("h", [128, KC, T], BF16)
        hh = sb("hh", [128, KC, 16], BF16)
        mh = sb("mh", [128, KC, T], BF16)
        wbf = [sb(f"wbf{i}", [128, KC, 128], BF16) for i in range(NW)]
        big = sb("big", [128, 2048], F32)
        xst = [big[:, i * 1024:(i + 1) * 1024] for i in range(2)]
        vext = [sb(f"vext{i}", [128, 528], F32) for i in range(2)]
        sA = sb("sA", [128, 528], F32)
        sB = sb("sB", [128, 528], F32)
        pooled = sb("pooled", [128, 4, T], BF16)
        a_g = big[:].rearrange("p (m t) -> p m t", m=4)
        cext = sA[:, 0:514]
        cuext = sB[:, 0:514]
        acc = vext[0][:, 0:T]
        Bo = vext[1][:, 0:T]
        sqbf = [sb(f"sqbf{i}", [128, T], BF16) for i in range(2)]
        sqh = sb("sqh", [128, KC, 16], BF16)
        rstd = [sb(f"rstd{i}", [128, T], F32) for i in range(2)]
        rstdh = sb("rstdh", [128, 16], F32)
        tmp = [sb(f"tmp{i}", [128, T], F32) for i in range(2)]
        tmph = sb("tmph", [128, KC, 16], F32)
        par = sb("par", [128, NPAR], F32)
        tab = sb("tab", [128, NT_FULL * 80], F32)
        ident = sb("ident", [128, 128], F32)
        ones = sb("ones", [128, 128], BF16)
        epsc = sb("epsc", [128, 1], F32)
        ct = sb("ct", [128, KC], F32)
        sbf = sb("sbf", [128, KC], BF16)
        mod = sb("mod", [128, 192], F32)
        gm = sb("gm", [128, 64], F32)
        banks = [st.enter_context(nc.psum_tensor(f"bank{i}", [128, 512], F32)) for i in range(8)]

        def build(P, wlist):
            def B1():
                return Buf(Region())
            R_xres = [Region() for _ in range(KC)]
            R_h = [Region() for _ in range(KC)]
            R_mh = [Region() for _ in range(KC)]
            b_xres = [Buf(r) for r in R_xres]
            b_h = [Buf(r) for r in R_h]
            b_mh = [Buf(r) for r in R_mh]
            b_xres_all, b_h_all, b_mh_all = Buf(R_xres), Buf(R_h), Buf(R_mh)
            b_xh, b_hh = B1(), B1()
            b_wbf = [B1() for _ in range(NW)]
            b_xst = [B1() for _ in range(2)]
            b_vext = [B1(), B1()]
            b_sA, b_sB, b_pooled = B1(), B1(), B1()
            b_cext, b_cuext, b_acc, b_Bo = b_sA, b_sB, b_vext[0], b_vext[1]
            b_ag = [b_xst[0], b_xst[0], b_xst[1], b_xst[1]]
            b_sq = [B1(), B1()]
            b_sqh, b_rstdh, b_tmph = B1(), B1(), B1()
            b_rstd = [B1(), B1()]
            b_tmp = [B1(), B1()]
            b_par, b_tab, b_ident, b_ones, b_eps, b_ct, b_sbf, b_mod, b_gm = (B1() for _ in range(9))
            b_bank = [Buf(Region(), excl=True) for _ in range(8)]
            rot = {"m": 0, "h": 0, "s": 0, "sq": 0, "rs": 0, "tp": 0, "ev": 0, "sqe": 0}

            def nxt(kind, n):
                v = rot[kind]
                rot[kind] = (v + 1) % n
                return v

            def main_bank():
                return nxt("m", 4)

            def halo_bank():
                return 4 + nxt("h", 2)

            def stat_bank():
                return 6 + nxt("s", 2)

            wstate = {"use": 0, "iss": 0}

            b_wsc = [B1() for _ in range(NWT)]
            wi = [0]

            def issue_w(n):
                view, nsl, sc = wlist[n]
                slot = n % NW
                ep = (n // NW) // 150
                flat = wbf[slot][:, 0:nsl, :].rearrange("p k c -> p (k c)")
                wb_at = 0 if sc is None else min(sc[1] % 3, max(NT - 2, 0))
                if sc is None or sc[0] <= wb_at:
                    out = wbf[slot][:, 0:nsl, :]
                    if view.shape[2] != 128:
                        out = wbf[slot][:, 0:nsl, :].rearrange("p (k m) c -> p k (m c)", k=view.shape[1])
                    P.dma("pool", f"w{slot}e{ep}", out, view, writes=[b_wbf[slot]])
                    if sc is not None and NT > 1 and sc[0] == wb_at:
                        P.dma("sp", f"wb{slot}e{ep}", wsc[sc[1]][:, 0:nsl * 128], flat, reads=[b_wbf[slot]], writes=[b_wsc[sc[1]]])
                else:
                    P.dma("pool", f"w{slot}e{ep}", flat, wsc[sc[1]][:, 0:nsl * 128], reads=[b_wsc[sc[1]]], writes=[b_wbf[slot]])

            def next_w(view, nsl=KC, tile_it=None):
                n = wstate["use"]
                wstate["use"] += 1
                sc = None
                if tile_it is not None:
                    sc = (tile_it, wi[0])
                    wi[0] += 1
                if P.dry:
                    wlist.append((view, nsl, sc))
                    return 0
                while wstate["iss"] < min(len(wlist), n + NW):
                    issue_w(wstate["iss"])
                    wstate["iss"] += 1
                return n % NW

            def evac_eng():
                return ("act", "dve")[nxt("ev", 2)]

            def copy_op(eng, out_ap, in_ap, reads, writes):
                if eng == "act":
                    P.op("act", lambda e: e.copy(out_ap, in_ap), reads=reads, writes=writes)
                else:
                    P.op("dve", lambda e: e.tensor_copy(out_ap, in_ap), reads=reads, writes=writes)

            def rsqrt_from_bank(bk, ncol, scale, dst_ap, dst_buf):
                P.op("act", lambda e: e.activation(dst_ap, banks[bk][:, 0:ncol], AF.Sqrt, bias=epsc[:, 0:1], scale=scale),
                     reads=[b_bank[bk], b_eps], writes=[dst_buf])
                P.op("dve", lambda e: e.reciprocal(dst_ap, dst_ap), reads=[dst_buf], writes=[dst_buf])

            def stat_chunk(sbk, c, first, last):
                q = nxt("sq", 2)
                if nxt("sqe", 2) == 0:
                    P.op("act", lambda e, c=c, q=q: e.activation(sqbf[q][:], xres[:, c, :], AF.Square),
                         reads=[b_xres[c]], writes=[b_sq[q]])
                else:
                    P.op("dve", lambda e, c=c, q=q: e.tensor_tensor(sqbf[q][:], xres[:, c, :], xres[:, c, :], ALU.mult),
                         reads=[b_xres[c]], writes=[b_sq[q]])
                P.op("pe", lambda e, q=q: e.matmul(banks[sbk][:], ones[:], sqbf[q][:], start=first, stop=last),
                     reads=[b_sq[q], b_ones], writes=[b_bank[sbk]], inc=True)

            def tile_load(it):
                s0 = it * T
                for g8 in range(4):
                    P.dma("sp", f"xl{g8}", xres[:, 8 * g8:8 * (g8 + 1), :], xv[:, 8 * g8:8 * (g8 + 1), 8 + s0:8 + s0 + T],
                          writes=[Buf(R_xres[8 * g8:8 * (g8 + 1)])])
                P.dma("sp", "xlh", xh[:, :, 0:8], xv[:, :, s0:s0 + 8], writes=[b_xh])
                P.dma("sp", "xlh", xh[:, :, 8:16], xv[:, :, 8 + s0 + T:16 + s0 + T], writes=[b_xh])
                sbk = stat_bank()
                for c in range(KC):
                    stat_chunk(sbk, c, first=(c == 0), last=(c == KC - 1))
                return sbk

            P.dma("sp", "par", par[:], pard, writes=[b_par])
            P.dma("sp", "ct", ct[:], ctd, writes=[b_ct])
            P.dma("sp", "tab", tab[:], tabd, writes=[b_tab])
            P.dma("sp", "ident", ident[:], identd, writes=[b_ident])
            P.op("dve", lambda e: e.memset(ones[:], 1.0), writes=[b_ones])
            P.op("dve", lambda e: e.memset(epsc[:], EPS), writes=[b_eps])
            P.op("act", lambda e: e.activation(sbf[:], ct[:], AF.Silu), reads=[b_ct], writes=[b_sbf])
            hoisted = [tile_load(0)]
            MODB = 5
            hf32 = h[:].rearrange("p k t -> p (k t)").bitcast(F32)
            mf32 = mh[:].rearrange("p k t -> p (k t)").bitcast(F32)
            stg = [hf32[:, 0:4096].rearrange("p (k c) -> p k c", k=8), hf32[:, 4096:8192].rearrange("p (k c) -> p k c", k=8),
                   mf32[:, 0:4096].rearrange("p (k c) -> p k c", k=8), mf32[:, 4096:8192].rearrange("p (k c) -> p k c", k=8)]
            b_stg = [Buf(R_h[0:16]), Buf(R_h[16:32]), Buf(R_mh[0:16]), Buf(R_mh[16:32])]
            cast_rot = ("act", "dve", "act", "dve", "act", "pool", "dve", "act")
            for cb in range(48):
                for kq in range(4):
                    n = cb * 4 + kq
                    sl = n % 4
                    P.dma("sp", f"ada{sl}", stg[sl], v_ada[:, kq * 8:(kq + 1) * 8, cb * 512:(cb + 1) * 512], writes=[b_stg[sl]])
                    wv = wbf[sl][:].rearrange("p (k m) c -> p k (m c)", k=8)
                    ce = cast_rot[n % len(cast_rot)]
                    if ce == "act":
                        P.op("act", lambda e, sl=sl, wv=wv: e.copy(wv, stg[sl]), reads=[b_stg[sl]], writes=[b_wbf[sl]])
                    else:
                        P.op(ce, lambda e, sl=sl, wv=wv: e.tensor_copy(wv, stg[sl]), reads=[b_stg[sl]], writes=[b_wbf[sl]])
                    for m in range(4):
                        for k in range(8):
                            col = cb * 4 + m
                            kk = kq * 8 + k
                            P.op("pe", lambda e, wv=wv, m=m, k=k, col=col, kk=kk, n=n, kq=kq: e.matmul(
                                banks[MODB][:, col:col + 1], wv[:, k, m * 128:(m + 1) * 128], sbf[:, kk:kk + 1],
                                start=(n == 0 and m == 0 and k == 0), stop=(kq == 3 and k == 7), skip_group_check=True),
                                 reads=[b_wbf[sl], b_sbf], writes=[b_bank[MODB]], inc=(m == 3 and k == 7))
            P.op("dve", lambda e: e.tensor_tensor(mod[:], banks[MODB][:, 0:192], par[:, O_BADA:O_BADA + 192], ALU.add),
                 reads=[b_bank[MODB], b_par], writes=[b_mod])
            P.op("dve", lambda e: e.scalar_tensor_tensor(gm[:, 0:32], mod[:, 32:64], 1.0, par[:, O_G1:O_G1 + 32], ALU.add, ALU.mult),
                 reads=[b_mod, b_par], writes=[b_gm])
            P.op("dve", lambda e: e.scalar_tensor_tensor(gm[:, 32:64], mod[:, 128:160], 1.0, par[:, O_G2:O_G2 + 32], ALU.add, ALU.mult),
                 reads=[b_mod, b_par, b_gm], writes=[b_gm])

            def norm_stats(src_bufs, halo):
                sbk = stat_bank()
                for c in range(KC):
                    q = nxt("sq", 2)
                    P.op("act", lambda e, c=c, q=q: e.activation(sqbf[q][:], xres[:, c, :], AF.Square),
                         reads=[src_bufs[c]], writes=[b_sq[q]])
                    P.op("pe", lambda e, c=c, q=q: e.matmul(banks[sbk][:], ones[:], sqbf[q][:], start=(c == 0), stop=(c == KC - 1)),
                         reads=[b_sq[q], b_ones], writes=[b_bank[sbk]], inc=True)
                hbk = None
                if halo:
                    hbk = halo_bank()
                    P.op("act", lambda e: e.activation(sqh[:], xh[:], AF.Square), reads=[b_xh], writes=[b_sqh])
                    for c in range(KC):
                        P.op("pe", lambda e, c=c: e.matmul(banks[hbk][:, 0:16], ones[:], sqh[:, c, :], start=(c == 0), stop=(c == KC - 1)),
                             reads=[b_sqh, b_ones], writes=[b_bank[hbk]], inc=(c == KC - 1))
                return sbk, hbk

            out_dmas = []
            def tile_body(it):
                s0 = it * T
                wi[0] = 0
                o_hm = it * 16
                o_corr = NT_FULL * 16 + it * 64

                sbk = tile_load(it) if it > 0 else hoisted[0]
                hbk = halo_bank()
                P.op("act", lambda e: e.activation(sqh[:], xh[:], AF.Square), reads=[b_xh], writes=[b_sqh])
                for c in range(KC):
                    P.op("pe", lambda e, c=c: e.matmul(banks[hbk][:, 0:16], ones[:], sqh[:, c, :], start=(c == 0), stop=(c == KC - 1)),
                         reads=[b_sqh, b_ones], writes=[b_bank[hbk]], inc=(c == KC - 1))
                r1 = nxt("rs", 2)
                rsqrt_from_bank(sbk, T, 1.0 / D, rstd[r1][:], b_rstd[r1])
                rsqrt_from_bank(hbk, 16, 1.0 / D, rstdh[:], b_rstdh)
                for c in range(KC):
                    q = nxt("tp", 2)
                    P.op("dve", lambda e, c=c, q=q: e.tensor_tensor(tmp[q][:], xres[:, c, :], rstd[r1][:], ALU.mult),
                         reads=[b_xres[c], b_rstd[r1]], writes=[b_tmp[q]])
                    P.op("act", lambda e, c=c, q=q: e.activation(h[:, c, :], tmp[q][:], AF.Identity, bias=mod[:, c:c + 1], scale=gm[:, c:c + 1]),
                         reads=[b_tmp[q], b_mod, b_gm], writes=[b_h[c]])
                P.op("dve", lambda e: e.tensor_tensor(tmph[:], xh[:], rstdh[:].unsqueeze(1).to_broadcast([128, KC, 16]), ALU.mult),
                     reads=[b_xh, b_rstdh], writes=[b_tmph])
                P.op("dve", lambda e: e.tensor_tensor(tmph[:], tmph[:], gm[:, 0:32].unsqueeze(2).to_broadcast([128, KC, 16]), ALU.mult),
                     reads=[b_tmph, b_gm], writes=[b_tmph])
                P.op("dve", lambda e: e.tensor_tensor(tmph[:], tmph[:], mod[:, 0:32].unsqueeze(2).to_broadcast([128, KC, 16]), ALU.add),
                     reads=[b_tmph, b_mod], writes=[b_tmph])
                P.op("dve", lambda e: e.tensor_tensor(hh[:], tmph[:], tab[:, o_hm:o_hm + 16].unsqueeze(1).to_broadcast([128, KC, 16]), ALU.mult),
                     reads=[b_tmph, b_tab], writes=[b_hh])

                def inproj_group(colblk, with_halo):
                    ws = next_w(v_in[:, :, colblk * 128:(colblk + 1) * 128], KC, it)
                    bk = main_bank()
                    hb_ = halo_bank() if with_halo else None
                    for k in range(KC):
                        last = (k == KC - 1)
                        P.op("pe", lambda e, k=k, ws=ws, bk=bk: e.matmul(banks[bk][:], wbf[ws][:, k, :], h[:, k, :], start=(k == 0), stop=(k == KC - 1)),
                             reads=[b_wbf[ws], b_h[k]], writes=[b_bank[bk]], inc=(last and not with_halo))
                        if with_halo:
                            P.op("pe", lambda e, k=k, ws=ws, hb_=hb_: e.matmul(banks[hb_][:, 0:16], wbf[ws][:, k, :], hh[:, k, :], start=(k == 0), stop=(k == KC - 1)),
                                 reads=[b_wbf[ws], b_hh], writes=[b_bank[hb_]], inc=last)
                    return bk, hb_

                def pool_post(g):
                    ws = next_w(v_pm[:, 4 * g:4 * g + 4, 0:512], 16, it)
                    sbk2 = stat_bank()
                    for m2 in range(4):
                        bk = main_bank()
                        for k in range(4):
                            P.op("pe", lambda e, k=k, m2=m2, ws=ws, bk=bk: e.matmul(banks[bk][:], wbf[ws][:, k * 4 + m2, :], pooled[:, k, :], start=(k == 0), stop=(k == 3)),
                                 reads=[b_wbf[ws], b_pooled], writes=[b_bank[bk]], inc=(k == 3))
                        cc = 4 * g + m2
                        P.op("act", lambda e, m2=m2, bk=bk, cc=cc: e.activation(a_g[:, m2, :], banks[bk][:], AF.Identity, scale=par[:, O_PSC + cc:O_PSC + cc + 1]),
                             reads=[b_bank[bk], b_par], writes=[b_ag[m2]])
                        q = nxt("sq", 2)
                        P.op("act", lambda e, m2=m2, q=q: e.activation(sqbf[q][:], a_g[:, m2, :], AF.Square), reads=[b_ag[m2]], writes=[b_sq[q]])
                        P.op("pe", lambda e, m2=m2, q=q: e.matmul(banks[sbk2][:], ones[:], sqbf[q][:], start=(m2 == 0), stop=(m2 == 3)),
                             reads=[b_sq[q], b_ones], writes=[b_bank[sbk2]], inc=True)
                    r = nxt("rs", 2)
                    rsqrt_from_bank(sbk2, T, 1.0 / 512, rstd[r][:], b_rstd[r])
                    for m2 in range(4):
                        cc = 4 * g + m2
                        P.op("dve", lambda e, m2=m2, cc=cc, r=r: e.scalar_tensor_tensor(mh[:, cc, :], a_g[:, m2, :], par[:, O_GPOOL + cc:O_GPOOL + cc + 1], rstd[r][:], ALU.mult, ALU.mult),
                             reads=[b_ag[m2], b_par, b_rstd[r]], writes=[b_mh[cc]])

                pend = []
                for g in range(4):
                    w = WINS[g]
                    half = w // 2
                    for m in range(4):
                        cc = 4 * g + m
                        bk, hb_ = inproj_group(cc, True)
                        if pend and pend[0][0] <= 0:
                            pend.pop(0)[1]()
                        pend = [(n - 1, f) for n, f in pend]
                        vi = cc % 2
                        P.op("act", lambda e, vi=vi, bk=bk: e.copy(vext[vi][:, 8:520], banks[bk][:]), reads=[b_bank[bk]], writes=[b_vext[vi]])
                        P.op("act", lambda e, vi=vi, hb_=hb_: e.copy(vext[vi][:, 0:8], banks[hb_][:, 0:8]), reads=[b_bank[hb_]], writes=[b_vext[vi]])
                        P.op("act", lambda e, vi=vi, hb_=hb_: e.copy(vext[vi][:, 520:528], banks[hb_][:, 8:16]), reads=[b_bank[hb_]], writes=[b_vext[vi]])
                        src, srcb, length = vext[vi], b_vext[vi], 528
                        dsts = [(sA, b_sA), (sB, b_sB)]
                        step, di = 1, 0
                        while step < w:
                            dst, dstb = dsts[di]
                            nl = length - step
                            P.op("dve", lambda e, src=src, dst=dst, nl=nl, step=step: e.tensor_tensor(dst[:, 0:nl], src[:, 0:nl], src[:, step:step + nl], ALU.add),
                                 reads=[srcb], writes=[dstb])
                            src, srcb, length = dst, dstb, nl
                            step *= 2
                            di ^= 1
                        lo = 8 - half
                        P.op("dve", lambda e, src=src, lo=lo, g=g: e.tensor_tensor(src[:, lo:lo + 8], src[:, lo:lo + 8], tab[:, o_corr + g * 16:o_corr + g * 16 + 8], ALU.mult),
                             reads=[srcb, b_tab], writes=[srcb])
                        P.op("dve", lambda e, src=src, lo=lo, g=g: e.tensor_tensor(src[:, lo + 504:lo + 512], src[:, lo + 504:lo + 512], tab[:, o_corr + g * 16 + 8:o_corr + g * 16 + 16], ALU.mult),
                             reads=[srcb, b_tab], writes=[srcb])
                        P.op("dve", lambda e, src=src, lo=lo, m=m, vi=vi, w=w: e.scalar_tensor_tensor(pooled[:, m, :], src[:, lo:lo + T], 1.0 / w, vext[vi][:, 8:520], ALU.mult, ALU.subtract),
                             reads=[srcb, b_vext[vi]], writes=[b_pooled])
                    pool_post(g)
                def conv_post(c):
                    sbk3 = stat_bank()
                    q = nxt("sq", 2)
                    P.op("act", lambda e, q=q: e.activation(sqbf[q][:], Bo, AF.Square), reads=[b_Bo], writes=[b_sq[q]])
                    P.op("pe", lambda e, q=q: e.matmul(banks[sbk3][:], ones[:], sqbf[q][:], start=True, stop=True),
                         reads=[b_sq[q], b_ones], writes=[b_bank[sbk3]], inc=True)
                    r = nxt("rs", 2)
                    rsqrt_from_bank(sbk3, T, 1.0 / 128, rstd[r][:], b_rstd[r])
                    P.op("dve", lambda e, c=c, r=r: e.scalar_tensor_tensor(mh[:, 16 + c, :], Bo, par[:, O_GCONV + c:O_GCONV + c + 1], rstd[r][:], ALU.mult, ALU.mult),
                         reads=[b_Bo, b_par, b_rstd[r]], writes=[b_mh[16 + c]])

                for c in range(16):
                    bkC, hbC = inproj_group(32 + c, True)
                    while pend and pend[0][0] <= 0:
                        pend.pop(0)[1]()
                    pend = [(n - 1, f) for n, f in pend]
                    P.op("act", lambda e, bkC=bkC: e.copy(cext[:, 1:513], banks[bkC][:]), reads=[b_bank[bkC]], writes=[b_cext])
                    P.op("act", lambda e, hbC=hbC: e.copy(cext[:, 0:514:513], banks[hbC][:, 7:9]), reads=[b_bank[hbC]], writes=[b_cext])
                    bkU, hbU = inproj_group(48 + c, True)
                    P.op("dve", lambda e, bkU=bkU: e.tensor_tensor(cuext[:, 1:513], banks[bkU][:], cext[:, 1:513], ALU.mult),
                         reads=[b_bank[bkU], b_cext], writes=[b_cuext])
                    P.op("dve", lambda e, hbU=hbU: e.tensor_tensor(cuext[:, 0:514:513], banks[hbU][:, 7:9], cext[:, 0:514:513], ALU.mult),
                         reads=[b_bank[hbU], b_cext], writes=[b_cuext])
                    P.op("act", lambda e, c=c: e.activation(acc, cuext[:, 0:512], AF.Identity, bias=par[:, O_CB + c:O_CB + c + 1], scale=par[:, O_CW + c:O_CW + c + 1]),
                         reads=[b_cuext, b_par], writes=[b_acc])
                    P.op("dve", lambda e, c=c: e.scalar_tensor_tensor(acc, cuext[:, 1:513], par[:, O_CW + 16 + c:O_CW + 16 + c + 1], acc, ALU.mult, ALU.add),
                         reads=[b_cuext, b_par, b_acc], writes=[b_acc])
                    P.op("dve", lambda e, c=c: e.scalar_tensor_tensor(acc, cuext[:, 2:514], par[:, O_CW + 32 + c:O_CW + 32 + c + 1], acc, ALU.mult, ALU.add),
                         reads=[b_cuext, b_par, b_acc], writes=[b_acc])
                    bkB, _ = inproj_group(16 + c, False)
                    P.op("dve", lambda e, bkB=bkB: e.tensor_tensor(Bo, banks[bkB][:], acc, ALU.mult),
                         reads=[b_bank[bkB], b_acc], writes=[b_Bo])
                    pend.append((0, lambda c=c: conv_post(c)))
                while pend:
                    pend.pop(0)[1]()

                def mblock(vw, krow0, colblk0, act, act_bufs, evac):
                    for kq in range(4):
                        ws = next_w(vw[:, krow0 + kq * 8: krow0 + (kq + 1) * 8, colblk0 * 128:(colblk0 + 4) * 128], KC, it)
                        wv = wbf[ws][:].rearrange("p (k m) c -> p k (m c)", k=8)
                        for m in range(4):
                            for k in range(8):
                                kk = kq * 8 + k
                                lastg = (kq == 3 and k == 7)
                                P.op("pe", lambda e, wv=wv, m=m, k=k, kk=kk, kq=kq: e.matmul(banks[m][:], wv[:, k, m * 128:(m + 1) * 128], act[:, kk, :],
                                                                                           start=(kq == 0 and k == 0), stop=(kq == 3 and k == 7)),
                                     reads=[b_wbf[ws], act_bufs[kk]], writes=[b_bank[m]], inc=(lastg or (m == 3 and k == 7)))
                            if kq == 3:
                                evac(m)

                for mb in range(KC // 4):
                    def ev_out(m, mb=mb):
                        j = 4 * mb + m
                        P.op("dve", lambda e, j=j, m=m: e.scalar_tensor_tensor(xres[:, j, :], banks[m][:], mod[:, 64 + j:65 + j], xres[:, j, :], ALU.mult, ALU.add),
                             reads=[b_bank[m], b_mod, b_xres[j]], writes=[b_xres[j]])
                    if mb == 0:
                        sbk_n2 = stat_bank()
                    mblock(v_out, 0, 4 * mb, mh, b_mh, ev_out)
                    if mb >= 1:
                        for m in range(4):
                            stat_chunk(sbk_n2, 4 * (mb - 1) + m, first=(mb == 1 and m == 0), last=False)
                for m in range(4):
                    stat_chunk(sbk_n2, KC - 4 + m, first=False, last=(m == 3))

                sbk = sbk_n2
                r2 = nxt("rs", 2)
                rsqrt_from_bank(sbk, T, 1.0 / D, rstd[r2][:], b_rstd[r2])
                for c in range(KC):
                    q = nxt("tp", 2)
                    P.op("dve", lambda e, c=c, q=q: e.tensor_tensor(tmp[q][:], xres[:, c, :], rstd[r2][:], ALU.mult),
                         reads=[b_xres[c], b_rstd[r2]], writes=[b_tmp[q]])
                    P.op("act", lambda e, c=c, q=q: e.activation(h[:, c, :], tmp[q][:], AF.Identity, bias=mod[:, 96 + c:97 + c], scale=gm[:, 32 + c:33 + c]),
                         reads=[b_tmp[q], b_mod, b_gm], writes=[b_h[c]])

                for hb in range(DFF // D):
                    for mb in range(KC // 4):
                        def ev_w1(m, mb=mb):
                            f = 4 * mb + m
                            q = nxt("tp", 2)
                            P.op("act", lambda e, q=q, m=m: e.activation(tmp[q][:], banks[m][:], AF.Relu), reads=[b_bank[m]], writes=[b_tmp[q]])
                            P.op("dve", lambda e, q=q, f=f: e.tensor_tensor(mh[:, f, :], tmp[q][:], tmp[q][:], ALU.mult), reads=[b_tmp[q]], writes=[b_mh[f]])
                        mblock(v_w1, 0, hb * KC + 4 * mb, h, b_h, ev_w1)
                    for mb in range(KC // 4):
                        def ev_w2(m, mb=mb):
                            j = 4 * mb + m
                            P.op("dve", lambda e, j=j, m=m: e.scalar_tensor_tensor(xres[:, j, :], banks[m][:], mod[:, 160 + j:161 + j], xres[:, j, :], ALU.mult, ALU.add),
                                 reads=[b_bank[m], b_mod, b_xres[j]], writes=[b_xres[j]])
                        lasthb = (hb == DFF // D - 1)
                        if lasthb and mb == 0:
                            sbk_fin = stat_bank()
                        mblock(v_w2, hb * KC, 4 * mb, mh, b_mh, ev_w2)
                        if lasthb and mb >= 1:
                            for m in range(4):
                                stat_chunk(sbk_fin, 4 * (mb - 1) + m, first=(mb == 1 and m == 0), last=False)
                for m in range(4):
                    stat_chunk(sbk_fin, KC - 4 + m, first=False, last=(m == 3))

                sbk = sbk_fin
                r3 = nxt("rs", 2)
                rsqrt_from_bank(sbk, T, 1.0 / D, rstd[r3][:], b_rstd[r3])
                for c in range(KC):
                    P.op("dve", lambda e, c=c: e.scalar_tensor_tensor(xres[:, c, :], xres[:, c, :], par[:, O_GF + c:O_GF + c + 1], rstd[r3][:], ALU.mult, ALU.mult),
                         reads=[b_xres[c], b_par, b_rstd[r3]], writes=[b_xres[c]])
                for g4 in range(8):
                    kv = P.dma("sp", f"yo{g4}", yv[:, 4 * g4:4 * (g4 + 1), s0:s0 + T], xres[:, 4 * g4:4 * (g4 + 1), :],
                               reads=[Buf(R_xres[4 * g4:4 * (g4 + 1)])])
                    out_dmas.append(kv)
            for it_ in range(NT):
                tile_body(it_)
            if not P.dry:
                P.wait_all("sp", [kv for kv in {k: v for k, v in out_dmas}.items()])

        wlist = []
        build(Prog(nc, dry=True), wlist)
        P = Prog(nc)
        build(P, wlist)
        P.emit()
    return nc


def _fm(v, n):
    return np.ascontiguousarray(np.asarray(v, np.float32).reshape(n, 128).T)


def _tables(half):
    t0 = half * TOK
    hm = np.zeros((NT_FULL, 16), np.float32)
    corr = np.ones((NT_FULL, 4, 16), np.float32)
    for it in range(NT_FULL):
        s0 = t0 + it * T
        for j in range(16):
            tok = s0 - 8 + j if j < 8 else s0 + T + (j - 8)
            hm[it, j] = 1.0 if 0 <= tok < S else 0.0
        for g, w in enumerate(WINS):
            hf = w // 2
            for j in range(16):
                tok = s0 + j if j < 8 else s0 + T - 16 + j
                cnt = min(tok + hf, S) - max(tok - hf, 0)
                corr[it, g, j] = float(w) / float(cnt)
    row = np.concatenate([hm.reshape(-1), corr.reshape(-1)])
    return np.ascontiguousarray(np.broadcast_to(row[None, :], (128, row.size))).astype(np.float32)


_NT = NT_FULL


def kernel(x, c, w_ada, b_ada, norm1_g, w_in, pool_mix_w, pool_scale, conv_w, conv_b,
           gnorm_pool_g, gnorm_conv_g, w_out, norm2_g, w_mlp_in, w_mlp_out, final_g):
    f32 = np.float32
    x = np.asarray(x, f32)
    c = np.asarray(c, f32)
    par = np.concatenate([
        _fm(np.asarray(b_ada)[0], 192), _fm(np.asarray(norm1_g)[0], 32), _fm(np.asarray(norm2_g)[0], 32),
        _fm(np.asarray(final_g), 32), _fm(np.asarray(pool_scale)[0], 16), _fm(np.asarray(gnorm_pool_g)[0], 16),
        _fm(np.asarray(conv_w)[0, 0], 16), _fm(np.asarray(conv_w)[0, 1], 16), _fm(np.asarray(conv_w)[0, 2], 16),
        _fm(np.asarray(conv_b)[0], 16), _fm(np.asarray(gnorm_conv_g)[0], 16)], axis=1)
    par = np.ascontiguousarray(par, f32)
    assert par.shape == (128, NPAR)
    shared = {
        "par": par,
        "ident": np.eye(128, dtype=f32),
        "w_ada": np.ascontiguousarray(np.asarray(w_ada, f32)[0]),
        "w_in": np.ascontiguousarray(np.asarray(w_in, f32)[0]),
        "pmw": np.ascontiguousarray(np.asarray(pool_mix_w, f32)[0].reshape(2048, 512)),
        "w_out": np.ascontiguousarray(np.asarray(w_out, f32)[0]),
        "w1": np.ascontiguousarray(np.asarray(w_mlp_in, f32)[0]),
        "w2": np.ascontiguousarray(np.asarray(w_mlp_out, f32)[0]),
    }
    in_maps = []
    for core in range(NCORE):
        b, half = core // 2, core % 2
        t0 = half * TOK
        xe = np.zeros((TOK + 16, D), f32)
        lo, hi = max(t0 - 8, 0), min(t0 + TOK + 8, S)
        xe[lo - (t0 - 8): hi - (t0 - 8)] = x[b, lo:hi]
        m = dict(shared)
        m["xeT"] = np.ascontiguousarray(xe.T)
        m["ct"] = _fm(c[b], KC)
        m["tab"] = _tables(half)
        in_maps.append(m)
    nc = build_nc(_NT)
    res = run_bass_kernel_spmd(nc, in_maps, core_ids=list(range(NCORE)))
    out = np.zeros((NB, S, D), f32)
    for core in range(NCORE):
        b, half = core // 2, core % 2
        out[b, half * TOK:(half + 1) * TOK] = res.results[core]["yT"].T
    return out
```

```python
import contextlib
import numpy as np
import concourse.bass as bass
import concourse.mybir as mybir
from concourse.bass_utils import run_bass_kernel_spmd

F32 = mybir.dt.float32
BF16 = mybir.dt.bfloat16
AF = mybir.ActivationFunctionType
ALU = mybir.AluOpType

D = 4096
S = 4096
NB = 4
NCORE = 8
TOK = 2048
T = 512
NT_FULL = TOK // T
KC = D // 128
DFF = 4 * D
EPS = 1e-6
WINS = (2, 4, 8, 16)
NW = 4
NPAR = 400
O_BADA, O_G1, O_G2, O_GF, O_PSC, O_GPOOL, O_CW, O_CB, O_GCONV = 0, 192, 224, 256, 288, 304, 320, 368, 384


class Region:
    __slots__ = ("w", "r")

    def __init__(self):
        self.w = None
        self.r = {}


class Buf:
    __slots__ = ("regs", "excl")

    def __init__(self, regs, excl=False):
        self.regs = regs if isinstance(regs, (list, tuple)) else [regs]
        self.excl = excl


class Prog:
    ENGS = ("pe", "act", "dve", "pool", "sp")

    def __init__(self, nc, dry=False):
        self.nc = nc
        self.dry = dry
        self.lists = {e: [] for e in self.ENGS}
        self.cnt = {e: 0 for e in self.ENGS}
        self.known = {e: {} for e in self.ENGS}
        self.clock = {}
        self.dma_cnt = {}

    def _deps(self, eng, reads, writes):
        needs = {}

        def need(kv):
            if kv is not None and needs.get(kv[0], 0) < kv[1]:
                needs[kv[0]] = kv[1]
        for b in reads:
            for rg in b.regs:
                need(rg.w)
        for b in writes:
            for rg in b.regs:
                need(rg.w)
                for kv in rg.r.items():
                    need(kv)
        kn = self.known[eng]
        waits = []
        for k, v in needs.items():
            if k == eng and eng in ("pe", "sp"):
                continue
            if kn.get(k, 0) >= v:
                continue
            waits.append((k, v))
        for k, v in waits:
            snap = self.clock.get((k, v))
            if snap:
                for kk, vv in snap.items():
                    if kn.get(kk, 0) < vv:
                        kn[kk] = vv
            if kn.get(k, 0) < v:
                kn[k] = v
        return waits

    def op(self, eng, fn, reads=(), writes=(), inc=True):
        if self.dry:
            return
        if any(b.excl for b in reads):
            writes = list(writes) + [b for b in reads if b.excl]
            reads = [b for b in reads if not b.excl]
        waits = self._deps(eng, reads, writes)
        self.lists[eng].append((waits, fn, inc))
        if inc:
            self.cnt[eng] += 1
            pos = self.cnt[eng]
            self.clock[(eng, pos)] = dict(self.known[eng])
        else:
            pos = self.cnt[eng] + 1
        for b in reads:
            for rg in b.regs:
                if rg.r.get(eng, 0) < pos:
                    rg.r[eng] = pos
        for b in writes:
            for rg in b.regs:
                rg.w = (eng, pos)
                rg.r = {}

    def dma(self, q, slot, out_ap, in_ap, reads=(), writes=()):
        if self.dry:
            return None
        waits = self._deps(q, reads, writes)
        n = self.dma_cnt.get(slot, 0) + 1
        self.dma_cnt[slot] = n
        key, val = ("dma", slot), 16 * n
        self.lists[q].append((waits, (out_ap, in_ap, slot), "dma"))
        self.clock[(key, val)] = dict(self.known[q])
        for b in reads:
            for rg in b.regs:
                if rg.r.get(key, 0) < val:
                    rg.r[key] = val
        for b in writes:
            for rg in b.regs:
                rg.w = (key, val)
                rg.r = {}
        return key, val

    def wait_all(self, eng, kvs):
        self.lists[eng].append((list(kvs), None, "nop"))

    def emit(self):
        nc = self.nc
        with contextlib.ExitStack() as st:
            sems = {}
            for e in self.ENGS:
                sems[e] = st.enter_context(nc.semaphore("s_" + e))
            for s in sorted(self.dma_cnt.keys()):
                sems[("dma", s)] = st.enter_context(nc.semaphore("d_" + s))
            block = st.enter_context(nc.Block())

            def run(name, eng):
                own = sems[name]
                for waits, fn, kind in self.lists[name]:
                    for k, v in waits:
                        eng.wait_ge(sems[k], v)
                    if kind == "dma":
                        eng.dma_start(out=fn[0], in_=fn[1]).then_inc(sems[("dma", fn[2])], 16)
                    elif kind == "nop":
                        pass
                    else:
                        ins = fn(eng)
                        if kind:
                            ins.then_inc(own, 1)

            @block.tensor
            def _(e):
                run("pe", e)

            @block.scalar
            def _(e):
                run("act", e)

            @block.vector
            def _(e):
                run("dve", e)

            @block.gpsimd
            def _(e):
                run("pool", e)

            @block.sync
            def _(e):
                run("sp", e)


def build_nc(NT):
    nc = bass.Bass("TRN2", target_bir_lowering=False)

    def din(name, shape):
        return nc.dram_tensor(name, shape, F32, kind="ExternalInput").ap()
    xeT = din("xeT", [D, TOK + 16])
    ctd = din("ct", [128, KC])
    pard = din("par", [128, NPAR])
    tabd = din("tab", [128, NT_FULL * 80])
    identd = din("ident", [128, 128])
    w_ada = din("w_ada", [D, 6 * D])
    w_in = din("w_in", [D, 2 * D])
    pmw = din("pmw", [2048, 512])
    w_out = din("w_out", [D, D])
    w1 = din("w1", [D, DFF])
    w2 = din("w2", [DFF, D])
    yT = nc.dram_tensor("yT", [D, TOK], F32, kind="ExternalOutput").ap()
    xv = xeT.rearrange("(c p) t -> p c t", p=128)
    yv = yT.rearrange("(c p) t -> p c t", p=128)
    NWT = 64 + 4 + 32 + 256
    wsc_parts = [nc.dram_tensor(f"wsc{i}", [NWT // 2, 128, KC * 128], BF16, kind="Internal").ap() for i in range(2)]

    class _Wsc:
        def __getitem__(self, i):
            return wsc_parts[i // (NWT // 2)][i % (NWT // 2)]
    wsc = _Wsc()

    def wview(w):
        return w.rearrange("(k p) e -> p k e", p=128)
    v_ada, v_in, v_pm, v_out, v_w1, v_w2 = (wview(w) for w in (w_ada, w_in, pmw, w_out, w1, w2))

    with contextlib.ExitStack() as st:
        def sb(name, shape, dt):
            return st.enter_context(nc.sbuf_tensor("sb_" + name, shape, dt))

        xres = sb("xres", [128, KC, T], F32)
        xh = sb("xh", [128, KC, 16], F32)
        h = sb("h", [128, KC, T], BF16)
        hh = sb("hh", [128, KC, 16], BF16)
        mh = sb("mh", [128, KC, T], BF16)
        wbf = [sb(f"wbf{i}", [128, KC, 128], BF16) for i in range(NW)]
        big = sb("big", [128, 2048], F32)
        xst = [big[:, i * 1024:(i + 1) * 1024] for i in range(2)]
        vext = [sb(f"vext{i}", [128, 528], F32) for i in range(2)]
        sA = sb("sA", [128, 528], F32)
        sB = sb("sB", [128, 528], F32)
        pooled2 = [sb(f"pooled{i}", [128, 4, T], BF16) for i in range(2)]
        a_g = big[:].rearrange("p (m t) -> p m t", m=4)
        cext = sA[:, 0:514]
        cuext = sB[:, 0:514]
        acc = vext[0][:, 0:T]
        Bo = vext[1][:, 0:T]
        sqbf = [sb(f"sqbf{i}", [128, T], BF16) for i in range(2)]
        sqh = sb("sqh", [128, KC, 16], BF16)
        rstd = [sb(f"rstd{i}", [128, T], F32) for i in range(2)]
        rstdh = sb("rstdh", [128, 16], F32)
        tmp = [sb(f"tmp{i}", [128, T], F32) for i in range(2)]
        tmph = sb("tmph", [128, KC, 16], F32)
        par = sb("par", [128, NPAR], F32)
        tab = sb("tab", [128, NT_FULL * 80], F32)
        ident = sb("ident", [128, 128], F32)
        ones = sb("ones", [128, 128], BF16)
        epsc = sb("epsc", [128, 1], F32)
        ct = sb("ct", [128, KC], F32)
        sbf = sb("sbf", [128, KC], BF16)
        mod = sb("mod", [128, 192], F32)
        gm = sb("gm", [128, 64], F32)
        banks = [st.enter_context(nc.psum_tensor(f"bank{i}", [128, 512], F32)) for i in range(8)]

        def build(P, wlist):
            def B1():
                return Buf(Region())
            R_xres = [Region() for _ in range(KC)]
            R_h = [Region() for _ in range(KC)]
            R_mh = [Region() for _ in range(KC)]
            b_xres = [Buf(r) for r in R_xres]
            b_h = [Buf(r) for r in R_h]
            b_mh = [Buf(r) for r in R_mh]
            b_xres_all, b_h_all, b_mh_all = Buf(R_xres), Buf(R_h), Buf(R_mh)
            b_xh, b_hh = B1(), B1()
            b_wbf = [B1() for _ in range(NW)]
            b_xst = [B1() for _ in range(2)]
            b_vext = [B1(), B1()]
            b_sA, b_sB = B1(), B1()
            b_pooled2 = [B1(), B1()]
            b_cext, b_cuext, b_acc, b_Bo = b_sA, b_sB, b_vext[0], b_vext[1]
            b_ag = [b_xst[0], b_xst[0], b_xst[1], b_xst[1]]
            b_sq = [B1(), B1()]
            b_sqh, b_rstdh, b_tmph = B1(), B1(), B1()
            b_rstd = [B1(), B1()]
            b_tmp = [B1(), B1()]
            b_par, b_tab, b_ident, b_ones, b_eps, b_ct, b_sbf, b_mod, b_gm = (B1() for _ in range(9))
            b_bank = [Buf(Region(), excl=True) for _ in range(8)]
            rot = {"m": 0, "h": 0, "s": 0, "sq": 0, "rs": 0, "tp": 0, "ev": 0, "sqe": 0}

            def nxt(kind, n):
                v = rot[kind]
                rot[kind] = (v + 1) % n
                return v

            def main_bank():
                return nxt("m", 4)

            def halo_bank():
                return 4 + nxt("h", 2)

            def stat_bank():
                return 6 + nxt("s", 2)

            wstate = {"use": 0, "iss": 0}

            b_wsc = [B1() for _ in range(NWT)]
            wi = [0]

            def issue_w(n):
                view, nsl, sc = wlist[n]
                slot = n % NW
                ep = (n // NW) // 150
                flat = wbf[slot][:, 0:nsl, :].rearrange("p k c -> p (k c)")
                wb_at = 0 if sc is None else min(sc[1] % 3, max(NT - 2, 0))
                if sc is None or sc[0] <= wb_at:
                    out = wbf[slot][:, 0:nsl, :]
                    if view.shape[2] != 128:
                        out = wbf[slot][:, 0:nsl, :].rearrange("p (k m) c -> p k (m c)", k=view.shape[1])
                    P.dma("pool", f"w{slot}e{ep}", out, view, writes=[b_wbf[slot]])
                    if sc is not None and NT > 1 and sc[0] == wb_at:
                        P.dma("sp", f"wb{slot}e{ep}", wsc[sc[1]][:, 0:nsl * 128], flat, reads=[b_wbf[slot]], writes=[b_wsc[sc[1]]])
                else:
                    P.dma("pool", f"w{slot}e{ep}", flat, wsc[sc[1]][:, 0:nsl * 128], reads=[b_wsc[sc[1]]], writes=[b_wbf[slot]])

            def next_w(view, nsl=KC, tile_it=None):
                n = wstate["use"]
                wstate["use"] += 1
                sc = None
                if tile_it is not None:
                    sc = (tile_it, wi[0])
                    wi[0] += 1
                if P.dry:
                    wlist.append((view, nsl, sc))
                    return 0
                while wstate["iss"] < min(len(wlist), n + NW):
                    issue_w(wstate["iss"])
                    wstate["iss"] += 1
                return n % NW

            def evac_eng():
                return ("act", "dve")[nxt("ev", 2)]

            def copy_op(eng, out_ap, in_ap, reads, writes):
                if eng == "act":
                    P.op("act", lambda e: e.copy(out_ap, in_ap), reads=reads, writes=writes)
                else:
                    P.op("dve", lambda e: e.tensor_copy(out_ap, in_ap), reads=reads, writes=writes)

            def rsqrt_from_bank(bk, ncol, scale, dst_ap, dst_buf):
                P.op("act", lambda e: e.activation(dst_ap, banks[bk][:, 0:ncol], AF.Sqrt, bias=epsc[:, 0:1], scale=scale),
                     reads=[b_bank[bk], b_eps], writes=[dst_buf])
                P.op("dve", lambda e: e.reciprocal(dst_ap, dst_ap), reads=[dst_buf], writes=[dst_buf])

            def stat_chunk(sbk, c, first, last):
                q = nxt("sq", 2)
                if nxt("sqe", 2) == 0:
                    P.op("act", lambda e, c=c, q=q: e.activation(sqbf[q][:], xres[:, c, :], AF.Square),
                         reads=[b_xres[c]], writes=[b_sq[q]])
                else:
                    P.op("dve", lambda e, c=c, q=q: e.tensor_tensor(sqbf[q][:], xres[:, c, :], xres[:, c, :], ALU.mult),
                         reads=[b_xres[c]], writes=[b_sq[q]])
                P.op("pe", lambda e, q=q: e.matmul(banks[sbk][:], ones[:], sqbf[q][:], start=first, stop=last),
                     reads=[b_sq[q], b_ones], writes=[b_bank[sbk]], inc=True)

            def tile_load(it):
                s0 = it * T
                for g8 in range(4):
                    P.dma("sp", f"xl{g8}", xres[:, 8 * g8:8 * (g8 + 1), :], xv[:, 8 * g8:8 * (g8 + 1), 8 + s0:8 + s0 + T],
                          writes=[Buf(R_xres[8 * g8:8 * (g8 + 1)])])
                P.dma("sp", "xlh", xh[:, :, 0:8], xv[:, :, s0:s0 + 8], writes=[b_xh])
                P.dma("sp", "xlh", xh[:, :, 8:16], xv[:, :, 8 + s0 + T:16 + s0 + T], writes=[b_xh])
                sbk = stat_bank()
                for c in range(KC):
                    stat_chunk(sbk, c, first=(c == 0), last=(c == KC - 1))
                return sbk

            P.dma("sp", "par", par[:], pard, writes=[b_par])
            P.dma("sp", "ct", ct[:], ctd, writes=[b_ct])
            P.dma("sp", "tab", tab[:], tabd, writes=[b_tab])
            P.dma("sp", "ident", ident[:], identd, writes=[b_ident])
            P.op("dve", lambda e: e.memset(ones[:], 1.0), writes=[b_ones])
            P.op("dve", lambda e: e.memset(epsc[:], EPS), writes=[b_eps])
            P.op("act", lambda e: e.activation(sbf[:], ct[:], AF.Silu), reads=[b_ct], writes=[b_sbf])
            hoisted = [tile_load(0)]
            MODB = 5
            hf32 = h[:].rearrange("p k t -> p (k t)").bitcast(F32)
            mf32 = mh[:].rearrange("p k t -> p (k t)").bitcast(F32)
            stg = [hf32[:, 0:4096].rearrange("p (k c) -> p k c", k=8), hf32[:, 4096:8192].rearrange("p (k c) -> p k c", k=8),
                   mf32[:, 0:4096].rearrange("p (k c) -> p k c", k=8), mf32[:, 4096:8192].rearrange("p (k c) -> p k c", k=8)]
            b_stg = [Buf(R_h[0:16]), Buf(R_h[16:32]), Buf(R_mh[0:16]), Buf(R_mh[16:32])]
            cast_rot = ("act", "dve", "act", "dve", "act", "pool", "dve", "act")
            for cb in range(48):
                for kq in range(4):
                    n = cb * 4 + kq
                    sl = n % 4
                    P.dma("sp", f"ada{sl}", stg[sl], v_ada[:, kq * 8:(kq + 1) * 8, cb * 512:(cb + 1) * 512], writes=[b_stg[sl]])
                    wv = wbf[sl][:].rearrange("p (k m) c -> p k (m c)", k=8)
                    ce = cast_rot[n % len(cast_rot)]
                    if ce == "act":
                        P.op("act", lambda e, sl=sl, wv=wv: e.copy(wv, stg[sl]), reads=[b_stg[sl]], writes=[b_wbf[sl]])
                    else:
                        P.op(ce, lambda e, sl=sl, wv=wv: e.tensor_copy(wv, stg[sl]), reads=[b_stg[sl]], writes=[b_wbf[sl]])
                    for m in range(4):
                        for k in range(8):
                            col = cb * 4 + m
                            kk = kq * 8 + k
                            P.op("pe", lambda e, wv=wv, m=m, k=k, col=col, kk=kk, n=n, kq=kq: e.matmul(
                                banks[MODB][:, col:col + 1], wv[:, k, m * 128:(m + 1) * 128], sbf[:, kk:kk + 1],
                                start=(n == 0 and m == 0 and k == 0), stop=(kq == 3 and k == 7), skip_group_check=True),
                                 reads=[b_wbf[sl], b_sbf], writes=[b_bank[MODB]], inc=(m == 3 and k == 7))
            P.op("dve", lambda e: e.tensor_tensor(mod[:], banks[MODB][:, 0:192], par[:, O_BADA:O_BADA + 192], ALU.add),
                 reads=[b_bank[MODB], b_par], writes=[b_mod])
            P.op("dve", lambda e: e.scalar_tensor_tensor(gm[:, 0:32], mod[:, 32:64], 1.0, par[:, O_G1:O_G1 + 32], ALU.add, ALU.mult),
                 reads=[b_mod, b_par], writes=[b_gm])
            P.op("dve", lambda e: e.scalar_tensor_tensor(gm[:, 32:64], mod[:, 128:160], 1.0, par[:, O_G2:O_G2 + 32], ALU.add, ALU.mult),
                 reads=[b_mod, b_par, b_gm], writes=[b_gm])

            def norm_stats(src_bufs, halo):
                sbk = stat_bank()
                for c in range(KC):
                    q = nxt("sq", 2)
                    P.op("act", lambda e, c=c, q=q: e.activation(sqbf[q][:], xres[:, c, :], AF.Square),
                         reads=[src_bufs[c]], writes=[b_sq[q]])
                    P.op("pe", lambda e, c=c, q=q: e.matmul(banks[sbk][:], ones[:], sqbf[q][:], start=(c == 0), stop=(c == KC - 1)),
                         reads=[b_sq[q], b_ones], writes=[b_bank[sbk]], inc=True)
                hbk = None
                if halo:
                    hbk = halo_bank()
                    P.op("act", lambda e: e.activation(sqh[:], xh[:], AF.Square), reads=[b_xh], writes=[b_sqh])
                    for c in range(KC):
                        P.op("pe", lambda e, c=c: e.matmul(banks[hbk][:, 0:16], ones[:], sqh[:, c, :], start=(c == 0), stop=(c == KC - 1)),
                             reads=[b_sqh, b_ones], writes=[b_bank[hbk]], inc=(c == KC - 1))
                return sbk, hbk

            out_dmas = []
            def tile_body(it):
                s0 = it * T
                wi[0] = 0
                o_hm = it * 16
                o_corr = NT_FULL * 16 + it * 64

                sbk = tile_load(it) if it > 0 else hoisted[0]
                hbk = halo_bank()
                P.op("act", lambda e: e.activation(sqh[:], xh[:], AF.Square), reads=[b_xh], writes=[b_sqh])
                for c in range(KC):
                    P.op("pe", lambda e, c=c: e.matmul(banks[hbk][:, 0:16], ones[:], sqh[:, c, :], start=(c == 0), stop=(c == KC - 1)),
                         reads=[b_sqh, b_ones], writes=[b_bank[hbk]], inc=(c == KC - 1))
                r1 = nxt("rs", 2)
                rsqrt_from_bank(sbk, T, 1.0 / D, rstd[r1][:], b_rstd[r1])
                rsqrt_from_bank(hbk, 16, 1.0 / D, rstdh[:], b_rstdh)
                for c in range(KC):
                    q = nxt("tp", 2)
                    P.op("dve", lambda e, c=c, q=q: e.tensor_tensor(tmp[q][:], xres[:, c, :], rstd[r1][:], ALU.mult),
                         reads=[b_xres[c], b_rstd[r1]], writes=[b_tmp[q]])
                    P.op("act", lambda e, c=c, q=q: e.activation(h[:, c, :], tmp[q][:], AF.Identity, bias=mod[:, c:c + 1], scale=gm[:, c:c + 1]),
                         reads=[b_tmp[q], b_mod, b_gm], writes=[b_h[c]])
                P.op("dve", lambda e: e.tensor_tensor(tmph[:], xh[:], rstdh[:].unsqueeze(1).to_broadcast([128, KC, 16]), ALU.mult),
                     reads=[b_xh, b_rstdh], writes=[b_tmph])
                P.op("dve", lambda e: e.tensor_tensor(tmph[:], tmph[:], gm[:, 0:32].unsqueeze(2).to_broadcast([128, KC, 16]), ALU.mult),
                     reads=[b_tmph, b_gm], writes=[b_tmph])
                P.op("dve", lambda e: e.tensor_tensor(tmph[:], tmph[:], mod[:, 0:32].unsqueeze(2).to_broadcast([128, KC, 16]), ALU.add),
                     reads=[b_tmph, b_mod], writes=[b_tmph])
                P.op("dve", lambda e: e.tensor_tensor(hh[:], tmph[:], tab[:, o_hm:o_hm + 16].unsqueeze(1).to_broadcast([128, KC, 16]), ALU.mult),
                     reads=[b_tmph, b_tab], writes=[b_hh])

                def inproj_group(colblk, with_halo):
                    ws = next_w(v_in[:, :, colblk * 128:(colblk + 1) * 128], KC, it)
                    bk = main_bank()
                    hb_ = halo_bank() if with_halo else None
                    for k in range(KC):
                        last = (k == KC - 1)
                        P.op("pe", lambda e, k=k, ws=ws, bk=bk: e.matmul(banks[bk][:], wbf[ws][:, k, :], h[:, k, :], start=(k == 0), stop=(k == KC - 1)),
                             reads=[b_wbf[ws], b_h[k]], writes=[b_bank[bk]], inc=(last and not with_halo))
                        if with_halo:
                            P.op("pe", lambda e, k=k, ws=ws, hb_=hb_: e.matmul(banks[hb_][:, 0:16], wbf[ws][:, k, :], hh[:, k, :], start=(k == 0), stop=(k == KC - 1)),
                                 reads=[b_wbf[ws], b_hh], writes=[b_bank[hb_]], inc=last)
                    return bk, hb_

                def pool_post(g):
                    ws = next_w(v_pm[:, 4 * g:4 * g + 4, 0:512], 16, it)
                    sbk2 = stat_bank()
                    for m2 in range(4):
                        bk = main_bank()
                        for k in range(4):
                            P.op("pe", lambda e, k=k, m2=m2, ws=ws, bk=bk, pg=pooled2[g % 2]: e.matmul(banks[bk][:], wbf[ws][:, k * 4 + m2, :], pg[:, k, :], start=(k == 0), stop=(k == 3)),
                                 reads=[b_wbf[ws], b_pooled2[g % 2]], writes=[b_bank[bk]], inc=(k == 3))
                        cc = 4 * g + m2
                        P.op("act", lambda e, m2=m2, bk=bk, cc=cc: e.activation(a_g[:, m2, :], banks[bk][:], AF.Identity, scale=par[:, O_PSC + cc:O_PSC + cc + 1]),
                             reads=[b_bank[bk], b_par], writes=[b_ag[m2]])
                        q = nxt("sq", 2)
                        P.op("act", lambda e, m2=m2, q=q: e.activation(sqbf[q][:], a_g[:, m2, :], AF.Square), reads=[b_ag[m2]], writes=[b_sq[q]])
                        P.op("pe", lambda e, m2=m2, q=q: e.matmul(banks[sbk2][:], ones[:], sqbf[q][:], start=(m2 == 0), stop=(m2 == 3)),
                             reads=[b_sq[q], b_ones], writes=[b_bank[sbk2]], inc=True)
                    r = nxt("rs", 2)
                    rsqrt_from_bank(sbk2, T, 1.0 / 512, rstd[r][:], b_rstd[r])
                    for m2 in range(4):
                        cc = 4 * g + m2
                        P.op("dve", lambda e, m2=m2, cc=cc, r=r: e.scalar_tensor_tensor(mh[:, cc, :], a_g[:, m2, :], par[:, O_GPOOL + cc:O_GPOOL + cc + 1], rstd[r][:], ALU.mult, ALU.mult),
                             reads=[b_ag[m2], b_par, b_rstd[r]], writes=[b_mh[cc]])

                pend = []
                for g in range(4):
                    w = WINS[g]
                    half = w // 2
                    for m in range(4):
                        cc = 4 * g + m
                        bk, hb_ = inproj_group(cc, True)
                        vi = cc % 2
                        P.op("act", lambda e, vi=vi, bk=bk: e.copy(vext[vi][:, 8:520], banks[bk][:]), reads=[b_bank[bk]], writes=[b_vext[vi]])
                        P.op("act", lambda e, vi=vi, hb_=hb_: e.copy(vext[vi][:, 0:8], banks[hb_][:, 0:8]), reads=[b_bank[hb_]], writes=[b_vext[vi]])
                        P.op("act", lambda e, vi=vi, hb_=hb_: e.copy(vext[vi][:, 520:528], banks[hb_][:, 8:16]), reads=[b_bank[hb_]], writes=[b_vext[vi]])
                        src, srcb, length = vext[vi], b_vext[vi], 528
                        dsts = [(sA, b_sA), (sB, b_sB)]
                        step, di = 1, 0
                        while step < w:
                            dst, dstb = dsts[di]
                            nl = length - step
                            P.op("dve", lambda e, src=src, dst=dst, nl=nl, step=step: e.tensor_tensor(dst[:, 0:nl], src[:, 0:nl], src[:, step:step + nl], ALU.add),
                                 reads=[srcb], writes=[dstb])
                            src, srcb, length = dst, dstb, nl
                            step *= 2
                            di ^= 1
                        lo = 8 - half
                        P.op("dve", lambda e, src=src, lo=lo, g=g: e.tensor_tensor(src[:, lo:lo + 8], src[:, lo:lo + 8], tab[:, o_corr + g * 16:o_corr + g * 16 + 8], ALU.mult),
                             reads=[srcb, b_tab], writes=[srcb])
                        P.op("dve", lambda e, src=src, lo=lo, g=g: e.tensor_tensor(src[:, lo + 504:lo + 512], src[:, lo + 504:lo + 512], tab[:, o_corr + g * 16 + 8:o_corr + g * 16 + 16], ALU.mult),
                             reads=[srcb, b_tab], writes=[srcb])
                        P.op("dve", lambda e, src=src, lo=lo, m=m, vi=vi, w=w, pg=pooled2[g % 2]: e.scalar_tensor_tensor(pg[:, m, :], src[:, lo:lo + T], 1.0 / w, vext[vi][:, 8:520], ALU.mult, ALU.subtract),
                             reads=[srcb, b_vext[vi]], writes=[b_pooled2[g % 2]])
                        if pend and pend[0][0] <= 0:
                            pend.pop(0)[1]()
                        pend = [(n - 1, f) for n, f in pend]
                    pend.append((1, lambda g=g: pool_post(g)))
                def conv_post(c):
                    sbk3 = stat_bank()
                    q = nxt("sq", 2)
                    P.op("act", lambda e, q=q: e.activation(sqbf[q][:], Bo, AF.Square), reads=[b_Bo], writes=[b_sq[q]])
                    P.op("pe", lambda e, q=q: e.matmul(banks[sbk3][:], ones[:], sqbf[q][:], start=True, stop=True),
                         reads=[b_sq[q], b_ones], writes=[b_bank[sbk3]], inc=True)
                    r = nxt("rs", 2)
                    rsqrt_from_bank(sbk3, T, 1.0 / 128, rstd[r][:], b_rstd[r])
                    P.op("dve", lambda e, c=c, r=r: e.scalar_tensor_tensor(mh[:, 16 + c, :], Bo, par[:, O_GCONV + c:O_GCONV + c + 1], rstd[r][:], ALU.mult, ALU.mult),
                         reads=[b_Bo, b_par, b_rstd[r]], writes=[b_mh[16 + c]])

                for c in range(16):
                    bkC, hbC = inproj_group(32 + c, True)
                    P.op("act", lambda e, bkC=bkC: e.copy(cext[:, 1:513], banks[bkC][:]), reads=[b_bank[bkC]], writes=[b_cext])
                    P.op("act", lambda e, hbC=hbC: e.copy(cext[:, 0:514:513], banks[hbC][:, 7:9]), reads=[b_bank[hbC]], writes=[b_cext])
                    while pend and pend[0][0] <= 0:
                        pend.pop(0)[1]()
                    pend = [(n - 1, f) for n, f in pend]
                    bkU, hbU = inproj_group(48 + c, True)
                    P.op("dve", lambda e, bkU=bkU: e.tensor_tensor(cuext[:, 1:513], banks[bkU][:], cext[:, 1:513], ALU.mult),
                         reads=[b_bank[bkU], b_cext], writes=[b_cuext])
                    P.op("dve", lambda e, hbU=hbU: e.tensor_tensor(cuext[:, 0:514:513], banks[hbU][:, 7:9], cext[:, 0:514:513], ALU.mult),
                         reads=[b_bank[hbU], b_cext], writes=[b_cuext])
                    P.op("act", lambda e, c=c: e.activation(acc, cuext[:, 0:512], AF.Identity, bias=par[:, O_CB + c:O_CB + c + 1], scale=par[:, O_CW + c:O_CW + c + 1]),
                         reads=[b_cuext, b_par], writes=[b_acc])
                    P.op("dve", lambda e, c=c: e.scalar_tensor_tensor(acc, cuext[:, 1:513], par[:, O_CW + 16 + c:O_CW + 16 + c + 1], acc, ALU.mult, ALU.add),
                         reads=[b_cuext, b_par, b_acc], writes=[b_acc])
                    P.op("dve", lambda e, c=c: e.scalar_tensor_tensor(acc, cuext[:, 2:514], par[:, O_CW + 32 + c:O_CW + 32 + c + 1], acc, ALU.mult, ALU.add),
                         reads=[b_cuext, b_par, b_acc], writes=[b_acc])
                    bkB, _ = inproj_group(16 + c, False)
                    P.op("dve", lambda e, bkB=bkB: e.tensor_tensor(Bo, banks[bkB][:], acc, ALU.mult),
                         reads=[b_bank[bkB], b_acc], writes=[b_Bo])
                    pend.append((0, lambda c=c: conv_post(c)))
                while pend:
                    pend.pop(0)[1]()

                def mblock(vw, krow0, colblk0, act, act_bufs, evac):
                    for kq in range(4):
                        ws = next_w(vw[:, krow0 + kq * 8: krow0 + (kq + 1) * 8, colblk0 * 128:(colblk0 + 4) * 128], KC, it)
                        wv = wbf[ws][:].rearrange("p (k m) c -> p k (m c)", k=8)
                        for m in range(4):
                            for k in range(8):
                                kk = kq * 8 + k
                                lastg = (kq == 3 and k == 7)
                                P.op("pe", lambda e, wv=wv, m=m, k=k, kk=kk, kq=kq: e.matmul(banks[m][:], wv[:, k, m * 128:(m + 1) * 128], act[:, kk, :],
                                                                                           start=(kq == 0 and k == 0), stop=(kq == 3 and k == 7)),
                                     reads=[b_wbf[ws], act_bufs[kk]], writes=[b_bank[m]], inc=(lastg or (m == 3 and k == 7)))
                            if kq == 3:
                                evac(m)

                for mb in range(KC // 4):
                    def ev_out(m, mb=mb):
                        j = 4 * mb + m
                        P.op("dve", lambda e, j=j, m=m: e.scalar_tensor_tensor(xres[:, j, :], banks[m][:], mod[:, 64 + j:65 + j], xres[:, j, :], ALU.mult, ALU.add),
                             reads=[b_bank[m], b_mod, b_xres[j]], writes=[b_xres[j]])
                    if mb == 0:
                        sbk_n2 = stat_bank()
                    mblock(v_out, 0, 4 * mb, mh, b_mh, ev_out)
                    if mb >= 1:
                        for m in range(4):
                            stat_chunk(sbk_n2, 4 * (mb - 1) + m, first=(mb == 1 and m == 0), last=False)
                for m in range(4):
                    stat_chunk(sbk_n2, KC - 4 + m, first=False, last=(m == 3))

                sbk = sbk_n2
                r2 = nxt("rs", 2)
                rsqrt_from_bank(sbk, T, 1.0 / D, rstd[r2][:], b_rstd[r2])
                for c in range(KC):
                    q = nxt("tp", 2)
                    P.op("dve", lambda e, c=c, q=q: e.tensor_tensor(tmp[q][:], xres[:, c, :], rstd[r2][:], ALU.mult),
                         reads=[b_xres[c], b_rstd[r2]], writes=[b_tmp[q]])
                    P.op("act", lambda e, c=c, q=q: e.activation(h[:, c, :], tmp[q][:], AF.Identity, bias=mod[:, 96 + c:97 + c], scale=gm[:, 32 + c:33 + c]),
                         reads=[b_tmp[q], b_mod, b_gm], writes=[b_h[c]])

                for hb in range(DFF // D):
                    for mb in range(KC // 4):
                        def ev_w1(m, mb=mb):
                            f = 4 * mb + m
                            q = nxt("tp", 2)
                            P.op("act", lambda e, q=q, m=m: e.activation(tmp[q][:], banks[m][:], AF.Relu), reads=[b_bank[m]], writes=[b_tmp[q]])
                            P.op("dve", lambda e, q=q, f=f: e.tensor_tensor(mh[:, f, :], tmp[q][:], tmp[q][:], ALU.mult), reads=[b_tmp[q]], writes=[b_mh[f]])
                        mblock(v_w1, 0, hb * KC + 4 * mb, h, b_h, ev_w1)
                    for mb in range(KC // 4):
                        def ev_w2(m, mb=mb):
                            j = 4 * mb + m
                            P.op("dve", lambda e, j=j, m=m: e.scalar_tensor_tensor(xres[:, j, :], banks[m][:], mod[:, 160 + j:161 + j], xres[:, j, :], ALU.mult, ALU.add),
                                 reads=[b_bank[m], b_mod, b_xres[j]], writes=[b_xres[j]])
                        lasthb = (hb == DFF // D - 1)
                        if lasthb and mb == 0:
                            sbk_fin = stat_bank()
                        mblock(v_w2, hb * KC, 4 * mb, mh, b_mh, ev_w2)
                        if lasthb and mb >= 1:
                            for m in range(4):
                                stat_chunk(sbk_fin, 4 * (mb - 1) + m, first=(mb == 1 and m == 0), last=False)
                for m in range(4):
                    stat_chunk(sbk_fin, KC - 4 + m, first=False, last=(m == 3))

                sbk = sbk_fin
                r3 = nxt("rs", 2)
                rsqrt_from_bank(sbk, T, 1.0 / D, rstd[r3][:], b_rstd[r3])
                for c in range(KC):
                    P.op("dve", lambda e, c=c: e.scalar_tensor_tensor(xres[:, c, :], xres[:, c, :], par[:, O_GF + c:O_GF + c + 1], rstd[r3][:], ALU.mult, ALU.mult),
                         reads=[b_xres[c], b_par, b_rstd[r3]], writes=[b_xres[c]])
                for g4 in range(8):
                    kv = P.dma("sp", f"yo{g4}", yv[:, 4 * g4:4 * (g4 + 1), s0:s0 + T], xres[:, 4 * g4:4 * (g4 + 1), :],
                               reads=[Buf(R_xres[4 * g4:4 * (g4 + 1)])])
                    out_dmas.append(kv)
            for it_ in range(NT):
                tile_body(it_)
            if not P.dry:
                P.wait_all("sp", [kv for kv in {k: v for k, v in out_dmas}.items()])

        wlist = []
        build(Prog(nc, dry=True), wlist)
        P = Prog(nc)
        build(P, wlist)
        P.emit()
    return nc


def _fm(v, n):
    return np.ascontiguousarray(np.asarray(v, np.float32).reshape(n, 128).T)


def _tables(half):
    t0 = half * TOK
    hm = np.zeros((NT_FULL, 16), np.float32)
    corr = np.ones((NT_FULL, 4, 16), np.float32)
    for it in range(NT_FULL):
        s0 = t0 + it * T
        for j in range(16):
            tok = s0 - 8 + j if j < 8 else s0 + T + (j - 8)
            hm[it, j] = 1.0 if 0 <= tok < S else 0.0
        for g, w in enumerate(WINS):
            hf = w // 2
            for j in range(16):
                tok = s0 + j if j < 8 else s0 + T - 16 + j
                cnt = min(tok + hf, S) - max(tok - hf, 0)
                corr[it, g, j] = float(w) / float(cnt)
    row = np.concatenate([hm.reshape(-1), corr.reshape(-1)])
    return np.ascontiguousarray(np.broadcast_to(row[None, :], (128, row.size))).astype(np.float32)


_NT = NT_FULL


def kernel(x, c, w_ada, b_ada, norm1_g, w_in, pool_mix_w, pool_scale, conv_w, conv_b,
           gnorm_pool_g, gnorm_conv_g, w_out, norm2_g, w_mlp_in, w_mlp_out, final_g):
    f32 = np.float32
    x = np.asarray(x, f32)
    c = np.asarray(c, f32)
    par = np.concatenate([
        _fm(np.asarray(b_ada)[0], 192), _fm(np.asarray(norm1_g)[0], 32), _fm(np.asarray(norm2_g)[0], 32),
        _fm(np.asarray(final_g), 32), _fm(np.asarray(pool_scale)[0], 16), _fm(np.asarray(gnorm_pool_g)[0], 16),
        _fm(np.asarray(conv_w)[0, 0], 16), _fm(np.asarray(conv_w)[0, 1], 16), _fm(np.asarray(conv_w)[0, 2], 16),
        _fm(np.asarray(conv_b)[0], 16), _fm(np.asarray(gnorm_conv_g)[0], 16)], axis=1)
    par = np.ascontiguousarray(par, f32)
    assert par.shape == (128, NPAR)
    shared = {
        "par": par,
        "ident": np.eye(128, dtype=f32),
        "w_ada": np.ascontiguousarray(np.asarray(w_ada, f32)[0]),
        "w_in": np.ascontiguousarray(np.asarray(w_in, f32)[0]),
        "pmw": np.ascontiguousarray(np.asarray(pool_mix_w, f32)[0].reshape(2048, 512)),
        "w_out": np.ascontiguousarray(np.asarray(w_out, f32)[0]),
        "w1": np.ascontiguousarray(np.asarray(w_mlp_in, f32)[0]),
        "w2": np.ascontiguousarray(np.asarray(w_mlp_out, f32)[0]),
    }
    in_maps = []
    for core in range(NCORE):
        b, half = core // 2, core % 2
        t0 = half * TOK
        xe = np.zeros((TOK + 16, D), f32)
        lo, hi = max(t0 - 8, 0), min(t0 + TOK + 8, S)
        xe[lo - (t0 - 8): hi - (t0 - 8)] = x[b, lo:hi]
        m = dict(shared)
        m["xeT"] = np.ascontiguousarray(xe.T)
        m["ct"] = _fm(c[b], KC)
        m["tab"] = _tables(half)
        in_maps.append(m)
    nc = build_nc(_NT)
    res = run_bass_kernel_spmd(nc, in_maps, core_ids=list(range(NCORE)))
    out = np.zeros((NB, S, D), f32)
    for core in range(NCORE):
        b, half = core // 2, core % 2
        out[b, half * TOK:(half + 1) * TOK] = res.results[core]["yT"].T
    return out
```
